# Optimizing a Trainium2 kernel written in Bass

```python
import math
import jax, jax.numpy as jnp
from jax import lax
import numpy as np

D_MODEL = 1024
BATCH = 1
SEQ = 16384
DEPTH = 4

GRID_W = 64
CTX_LEN = 256
N_MIXERS = 3
NORM_EPS = 1e-6
ROPE_THETA = 10000.0
Q_BLOCK = 128

A_HEADS = 8
A_HEAD_DIM = 64
A_WIDTH = A_HEADS * 2 * A_HEAD_DIM
B_HEADS = 16
B_HEAD_DIM = 64
B_WIDTH = B_HEADS * B_HEAD_DIM
NA_KH = 8
NA_KW = 16
NA_ROW_BLOCK = 2
C_HEADS = 16
C_KV_HEADS = 4
C_GROUP = C_HEADS // C_KV_HEADS
C_HEAD_DIM = 64
C_WINDOW = 128
C_QKV = C_HEADS * C_HEAD_DIM + 2 * C_KV_HEADS * C_HEAD_DIM
FFN_HIDDEN = ((8 * D_MODEL + 3 * 256 - 1) // (3 * 256)) * 256
N_A = (DEPTH + 2) // 3
N_B = (DEPTH + 1) // 3
N_C = DEPTH // 3

kernel_name = "hybrid_diff_na_swa_dit_trunk"


def rmsnorm(x, g):
    xf = x.astype(jnp.float32)
    y = xf * lax.rsqrt(jnp.mean(xf * xf, axis=-1, keepdims=True) + NORM_EPS)
    return (y * g.astype(jnp.float32)).astype(x.dtype)


def grid_angles(n_tok, rot_dim):
    half = rot_dim // 2
    freqs = ROPE_THETA ** (-jnp.arange(0, half, 2, dtype=jnp.float32) / half)
    pos = jnp.arange(n_tok)
    rows = (pos // GRID_W).astype(jnp.float32)
    cols = (pos % GRID_W).astype(jnp.float32)
    return rows[:, None] * freqs, cols[:, None] * freqs


def rotate(x, ang):
    cos = jnp.cos(ang)[:, None, :].astype(x.dtype)
    sin = jnp.sin(ang)[:, None, :].astype(x.dtype)
    x1, x2 = jnp.split(x, 2, axis=-1)
    return jnp.concatenate([x1 * cos - x2 * sin, x2 * cos + x1 * sin], axis=-1)


def rope_2d(x, ang_row, ang_col):
    xr, xc = jnp.split(x, 2, axis=-1)
    return jnp.concatenate([rotate(xr, ang_row), rotate(xc, ang_col)], axis=-1)


def diff_attention(h_lat, h_ctx, wqkv, wo, lam, subln, lam_init, ang, with_ctx_out):
    B, S, _ = h_lat.shape
    H, d = A_HEADS, A_HEAD_DIM
    scale = d ** -0.5

    def project(h):
        n = h.shape[1]
        q, k, v = jnp.split(h @ wqkv, 3, axis=-1)
        return q.reshape(B, n, H, 2, d), k.reshape(B, n, H, 2, d), v.reshape(B, n, H, 2 * d)

    q, k, v = project(h_lat)
    qc, kc, vc = project(h_ctx)
    q = rope_2d(q.reshape(B, S, 2 * H, d), *ang).reshape(B, S, H, 2, d) * scale
    k = rope_2d(k.reshape(B, S, 2 * H, d), *ang).reshape(B, S, H, 2, d)
    lam_full = (jnp.exp(jnp.sum(lam[0] * lam[1])) - jnp.exp(jnp.sum(lam[2] * lam[3])) + lam_init).astype(jnp.float32)

    def attend(qb, kk, vv):
        s = jnp.einsum('bqhmd,bkhmd->bhmqk', qb, kk).astype(jnp.float32)
        p = jax.nn.softmax(s, axis=-1)
        a = (p[:, :, 0] - lam_full * p[:, :, 1]).astype(vv.dtype)
        return jnp.einsum('bhqk,bkhe->bqhe', a, vv)

    def finish(o):
        n = o.shape[1]
        return (rmsnorm(o, subln) * (1.0 - lam_init)).reshape(B, n, A_WIDTH) @ wo

    keys = jnp.concatenate([kc, k], axis=1)
    vals = jnp.concatenate([vc, v], axis=1)
    nb = S // Q_BLOCK
    qblk = q.reshape(B, nb, Q_BLOCK, H, 2, d).swapaxes(0, 1)
    o = lax.map(lambda qb: attend(qb, keys, vals), qblk)
    y_lat = finish(o.swapaxes(0, 1).reshape(B, S, H, 2 * d))
    y_ctx = finish(attend(qc * scale, kc, vc)) if with_ctx_out else None
    return y_lat, y_ctx


def neighbourhood_attention(h_lat, h_ctx, wqkv, wo, rpb, with_ctx_out):
    B, S, _ = h_lat.shape
    H, d = B_HEADS, B_HEAD_DIM
    scale = d ** -0.5
    rows = S // GRID_W
    kh = min(NA_KH, rows)
    kw = NA_KW

    def project(h):
        n = h.shape[1]
        q, k, v = jnp.split(h @ wqkv, 3, axis=-1)
        return q.reshape(B, n, H, d), k.reshape(B, n, H, d), v.reshape(B, n, H, d)

    q, k, v = project(h_lat)
    qc, kc, vc = project(h_ctx)
    q = q * scale

    r = jnp.arange(rows)
    cg = jnp.arange(GRID_W)
    krow = jnp.clip(r - kh // 2, 0, rows - kh)[:, None] + jnp.arange(kh)
    kcol = jnp.clip(cg - kw // 2, 0, GRID_W - kw)[:, None] + jnp.arange(kw)
    nk = kh * kw
    nbr = (krow[:, None, :, None] * GRID_W + kcol[None, :, None, :]).reshape(S, nk)
    drow = krow - r[:, None] + (NA_KH - 1)
    dcol = kcol - cg[:, None] + (NA_KW - 1)
    rel = (drow[:, None, :, None] * (2 * NA_KW - 1) + dcol[None, :, None, :]).reshape(S, nk)
    rpb_flat = rpb.reshape(H, -1).astype(jnp.float32)

    qb_len = NA_ROW_BLOCK * GRID_W
    nb = rows // NA_ROW_BLOCK

    def block(args):
        qb, idx, rl = args
        kg = jnp.take(k, idx, axis=1)
        vg = jnp.take(v, idx, axis=1)
        s_nb = jnp.einsum('bqhd,bqnhd->bhqn', qb, kg).astype(jnp.float32) + rpb_flat[:, rl]
        s_cx = jnp.einsum('bqhd,bkhd->bhqk', qb, kc).astype(jnp.float32)
        p = jax.nn.softmax(jnp.concatenate([s_nb, s_cx], axis=-1), axis=-1).astype(v.dtype)
        return (jnp.einsum('bhqn,bqnhd->bqhd', p[..., :nk], vg)
                + jnp.einsum('bhqk,bkhd->bqhd', p[..., nk:], vc))

    qblk = q.reshape(B, nb, qb_len, H, d).swapaxes(0, 1)
    o = lax.map(block, (qblk, nbr.reshape(nb, qb_len, nk), rel.reshape(nb, qb_len, nk)))
    y_lat = o.swapaxes(0, 1).reshape(B, S, B_WIDTH) @ wo
    y_ctx = None
    if with_ctx_out:
        s = jnp.einsum('bqhd,bkhd->bhqk', qc * scale, kc).astype(jnp.float32)
        p = jax.nn.softmax(s, axis=-1).astype(vc.dtype)
        y_ctx = jnp.einsum('bhqk,bkhd->bqhd', p, vc).reshape(B, -1, B_WIDTH) @ wo
    return y_lat, y_ctx


def window_gqa(h_lat, h_ctx, wqkv, wo, sink, ang, with_ctx_out):
    B, S, _ = h_lat.shape
    H, KV, G, d = C_HEADS, C_KV_HEADS, C_GROUP, C_HEAD_DIM
    QB = Q_BLOCK
    C = h_ctx.shape[1]
    scale = d ** -0.5

    def project(h):
        n = h.shape[1]
        q, k, v = jnp.split(h @ wqkv, [H * d, H * d + KV * d], axis=-1)
        return q.reshape(B, n, H, d), k.reshape(B, n, KV, d), v.reshape(B, n, KV, d)

    q, k, v = project(h_lat)
    qc, kc, vc = project(h_ctx)
    q = rope_2d(q, *ang) * scale
    k = rope_2d(k, *ang)
    sink_f = sink.astype(jnp.float32).reshape(KV, G, 1, 1)

    nb = S // QB
    pad = ((0, 0), (QB, QB), (0, 0), (0, 0))
    kb = jnp.pad(k, pad).reshape(B, nb + 2, QB, KV, d)
    vb = jnp.pad(v, pad).reshape(B, nb + 2, QB, KV, d)
    k_band = jnp.concatenate([kb[:, :-2], kb[:, 1:-1], kb[:, 2:]], axis=2).swapaxes(0, 1)
    v_band = jnp.concatenate([vb[:, :-2], vb[:, 1:-1], vb[:, 2:]], axis=2).swapaxes(0, 1)
    qblk = q.reshape(B, nb, QB, KV, G, d).swapaxes(0, 1)
    nband = 3 * QB

    def block(args):
        qb, kk, vv, i = args
        qpos = i * QB + jnp.arange(QB)
        kpos = (i - 1) * QB + jnp.arange(nband)
        valid = (jnp.abs(qpos[:, None] - kpos[None, :]) <= C_WINDOW) & (kpos >= 0)[None, :] & (kpos < S)[None, :]
        s_band = jnp.einsum('bqkgd,bnkd->bkgqn', qb, kk).astype(jnp.float32)
        s_band = jnp.where(valid, s_band, -jnp.inf)
        s_cx = jnp.einsum('bqkgd,bnkd->bkgqn', qb, kc).astype(jnp.float32)
        snk = jnp.broadcast_to(sink_f, (B, KV, G, QB, 1))
        p = jax.nn.softmax(jnp.concatenate([s_band, s_cx, snk], axis=-1), axis=-1).astype(v.dtype)
        return (jnp.einsum('bkgqn,bnkd->bqkgd', p[..., :nband], vv)
                + jnp.einsum('bkgqn,bnkd->bqkgd', p[..., nband:nband + C], vc))

    o = lax.map(block, (qblk, k_band, v_band, jnp.arange(nb)))
    y_lat = o.swapaxes(0, 1).reshape(B, S, H * d) @ wo
    y_ctx = None
    if with_ctx_out:
        s = jnp.einsum('bqkgd,bnkd->bkgqn', qc.reshape(B, C, KV, G, d) * scale, kc).astype(jnp.float32)
        snk = jnp.broadcast_to(sink_f, (B, KV, G, C, 1))
        p = jax.nn.softmax(jnp.concatenate([s, snk], axis=-1), axis=-1).astype(vc.dtype)
        y_ctx = jnp.einsum('bkgqn,bnkd->bqkgd', p[..., :C], vc).reshape(B, C, H * d) @ wo
    return y_lat, y_ctx


def swiglu(h, w13, w2):
    gate, up = jnp.split(h @ w13, 2, axis=-1)
    return (jax.nn.silu(gate) * up) @ w2


def setup_inputs(seed: int = 0) -> dict:
    key = jax.random.key(seed)
    ks = jax.random.split(key, 19)
    D = D_MODEL

    def nrm(k, shape):
        return jax.random.normal(k, shape, jnp.float32)

    def w(k, shape, fan_in):
        return nrm(k, shape) * fan_in ** -0.5

    return {
        "x": nrm(ks[0], (BATCH, SEQ, D)),
        "c": nrm(ks[1], (BATCH, D)),
        "ctx": nrm(ks[2], (BATCH, CTX_LEN, D)),
        "c_ctx": nrm(ks[3], (D,)),
        "ada_w": w(ks[4], (DEPTH, D, 6 * D), D),
        "ada_b": 0.02 * nrm(ks[5], (DEPTH, 6 * D)),
        "norm_g": 1.0 + 0.1 * nrm(ks[6], (DEPTH, 4, D)),
        "ffn_w13": w(ks[7], (DEPTH, D, 2 * FFN_HIDDEN), D),
        "ffn_w2": w(ks[8], (DEPTH, FFN_HIDDEN, D), FFN_HIDDEN),
        "a_wqkv": w(ks[9], (N_A, D, 3 * A_WIDTH), D),
        "a_wo": w(ks[10], (N_A, A_WIDTH, D), A_WIDTH),
        "a_lambda": 0.1 * nrm(ks[11], (N_A, 4, A_HEAD_DIM)),
        "a_subln": 1.0 + 0.1 * nrm(ks[12], (N_A, 2 * A_HEAD_DIM)),
        "b_wqkv": w(ks[13], (N_B, D, 3 * B_WIDTH), D),
        "b_wo": w(ks[14], (N_B, B_WIDTH, D), B_WIDTH),
        "b_rpb": 0.5 * nrm(ks[15], (N_B, B_HEADS, 2 * NA_KH - 1, 2 * NA_KW - 1)),
        "c_wqkv": w(ks[16], (N_C, D, C_QKV), D),
        "c_wo": w(ks[17], (N_C, C_HEADS * C_HEAD_DIM, D), C_HEADS * C_HEAD_DIM),
        "c_sink": nrm(ks[18], (N_C, C_HEADS)),
    }


def reference(x, c, ctx, c_ctx, ada_w, ada_b, norm_g, ffn_w13, ffn_w2,
              a_wqkv, a_wo, a_lambda, a_subln, b_wqkv, b_wo, b_rpb, c_wqkv, c_wo, c_sink):
    S = x.shape[1]
    ang = grid_angles(S, A_HEAD_DIM)
    s_lat = jax.nn.silu(c)
    s_ctx = jax.nn.silu(c_ctx)
    for i in range(DEPTH):
        last = i == DEPTH - 1
        mixer, j = i % N_MIXERS, i // N_MIXERS
        g = norm_g[i]
        mod = (s_lat @ ada_w[i] + ada_b[i])[:, None, :]
        sh1, sc1, gt1, sh2, sc2, gt2 = jnp.split(mod, 6, axis=-1)
        modc = s_ctx @ ada_w[i] + ada_b[i]
        csh1, csc1, cgt1, csh2, csc2, cgt2 = jnp.split(modc, 6, axis=-1)

        h = rmsnorm(x, g[0]) * (1.0 + sc1) + sh1
        hc = rmsnorm(ctx, g[0]) * (1.0 + csc1) + csh1
        if mixer == 0:
            lam_init = 0.8 - 0.6 * math.exp(-0.3 * i)
            y, yc = diff_attention(h, hc, a_wqkv[j], a_wo[j], a_lambda[j], a_subln[j], lam_init, ang, not last)
        elif mixer == 1:
            y, yc = neighbourhood_attention(h, hc, b_wqkv[j], b_wo[j], b_rpb[j], not last)
        else:
            y, yc = window_gqa(h, hc, c_wqkv[j], c_wo[j], c_sink[j], ang, not last)

        x = x + gt1 * rmsnorm(y, g[1])
        h = rmsnorm(x, g[2]) * (1.0 + sc2) + sh2
        x = x + gt2 * rmsnorm(swiglu(h, ffn_w13[i], ffn_w2[i]), g[3])
        if not last:
            ctx = ctx + cgt1 * rmsnorm(yc, g[1])
            hc = rmsnorm(ctx, g[2]) * (1.0 + csc2) + csh2
            ctx = ctx + cgt2 * rmsnorm(swiglu(hc, ffn_w13[i], ffn_w2[i]), g[3])
    return x
```

```python
import math
from contextlib import ExitStack

import numpy as np
import ml_dtypes

import concourse.bass as bass
import concourse.mybir as mybir
from concourse.bass_utils import run_bass_kernel_spmd

F32 = mybir.dt.float32
BF16 = mybir.dt.bfloat16
AF = mybir.ActivationFunctionType
ALU = mybir.AluOpType
AX = mybir.AxisListType

NCORES = 8
D = 1024
SEQ = 16384
CTX = 256
TPC = SEQ // NCORES
NLT = TPC // 128
NCT = CTX // 128
NTT = NLT + NCT
DEPTH = 4
FFN = 2816
EPS = 1e-6
GRID_W = 64
NDMA_SEM = 12


class Prog:
    def __init__(self, nc, es):
        self.nc = nc
        self.es = es
        self.es_root = es
        self.ops = []
        self.lastw = {}
        self.readers = {}
        self.eng = {"pe": nc.tensor, "act": nc.scalar, "dve": nc.vector, "pool": nc.gpsimd, "sp": nc.sync}
        self.out_dmas = []

    def add(self, eng, fn, r=(), w=(), dma=False):
        idx = len(self.ops)
        deps = set()
        psk = [k for k in list(r) + list(w) if isinstance(k, str) and k.startswith("PS:")]
        r = [k for k in r if k not in psk]
        w = [k for k in w if k not in psk]

        def need(di, kind):
            d = self.ops[di]
            if d["dma"] or dma:
                return True
            if d["eng"] == eng:
                return eng != "pe"
            return True

        for k in psk:
            lw = self.lastw.get(k)
            if lw is not None and self.ops[lw]["eng"] != eng:
                deps.add(lw)
            self.lastw[k] = idx

        for k in r:
            lw = self.lastw.get(k)
            if lw is not None and need(lw, "raw"):
                deps.add(lw)
        for k in w:
            lw = self.lastw.get(k)
            if lw is not None and need(lw, "waw"):
                deps.add(lw)
            for rd in self.readers.get(k, ()):
                if rd != idx and need(rd, "war"):
                    deps.add(rd)
        for k in r:
            lst = self.readers.setdefault(k, [])
            if not dma:
                lst[:] = [j for j in lst if self.ops[j]["dma"] or self.ops[j]["eng"] != eng]
            lst.append(idx)
        for k in w:
            self.lastw[k] = idx
            self.readers[k] = []
        for d in deps:
            self.ops[d]["mark"] = True
        self.ops.append(dict(eng=eng, fn=fn, dma=dma, deps=deps, mark=False))
        return idx

    def pe(self, fn, r=(), w=()):
        return self.add("pe", fn, r, w)

    def act(self, fn, r=(), w=()):
        return self.add("act", fn, r, w)

    def dve(self, fn, r=(), w=()):
        return self.add("dve", fn, r, w)

    def pool(self, fn, r=(), w=()):
        return self.add("pool", fn, r, w)

    def dma(self, q, out, in_, r=(), w=(), is_output=False, **kw):
        i = self.add(q, lambda e: e.dma_start(out=out, in_=in_, **kw), r, w, dma=True)
        if is_output:
            self.out_dmas.append(i)
        return i

    def cc(self, fn, r=(), w=()):
        i = self.add("pool", fn, r, w, dma=True)
        self.ops[i]["cc"] = True
        return i

    def barrier(self):
        n = len(self.ops)
        last = {}
        for i, op in enumerate(self.ops):
            if op["fn"] is None:
                continue
            if op["dma"]:
                last.setdefault(("dma", i), i)
            else:
                last[op["eng"]] = i
        prev = getattr(self, "_last_barrier", 0)
        deps = set()
        for k, i in last.items():
            if isinstance(k, tuple):
                if i >= prev:
                    deps.add(i)
            else:
                deps.add(i)
        for d in deps:
            self.ops[d]["mark"] = True
        for e in ("pe", "act", "dve", "pool", "sp"):
            self.ops.append(dict(eng=e, fn=None, dma=False, deps=set(deps), mark=False))
        self._last_barrier = n
        self.lastw = {}
        self.readers = {}

    def finish(self):
        deps = set(self.out_dmas)
        last = {}
        for i, op in enumerate(self.ops):
            if op["fn"] is not None:
                last[op["eng"]] = i
        deps.update(last.values())
        for d in deps:
            self.ops[d]["mark"] = True
        self.ops.append(dict(eng="sp", fn=None, dma=False, deps=deps, mark=False))
        self.emit()

    def emit(self):
        nc, es = self.nc, self.es_root
        if not hasattr(self, "_st"):
            sems = {e: es.enter_context(nc.semaphore(f"sem_{e}")) for e in ("pe", "act", "dve", "pool", "cc")}
            dsem = {q: [es.enter_context(nc.semaphore(f"dsem_{q}{i}")) for i in range(NDMA_SEM)]
                    for q in ("sp", "pool", "act")}
            semobj = {}
            for e, s in sems.items():
                semobj[f"c_{e}"] = s
            for q, lst in dsem.items():
                for i, s in enumerate(lst):
                    semobj[f"d_{q}{i}"] = s
            self._st = dict(sems=sems, semobj=semobj, cnt={e: 0 for e in sems},
                            dman={q: 0 for q in dsem}, waited={}, done={}, pos=0)
        st = self._st
        sems, semobj, cnt, dman, waited, done = (st["sems"], st["semobj"], st["cnt"], st["dman"],
                                                 st["waited"], st["done"])
        start = st["pos"]
        st["pos"] = len(self.ops)
        for i in range(start, len(self.ops)):
            op = self.ops[i]
            ename = op["eng"]
            eng = self.eng[ename]
            waits = {}
            for d in op["deps"]:
                sname, val = done[d]
                if waits.get(sname, 0) < val:
                    waits[sname] = val
            if op.get("cc"):
                pass
            elif op["dma"]:
                m = dman[ename]
                dman[ename] += 1
                sname = f"d_{ename}{m % NDMA_SEM}"
                val = 16 * (m // NDMA_SEM + 1)
                if m >= NDMA_SEM and waits.get(sname, 0) < val - 16:
                    waits[sname] = val - 16
            for sn, v in waits.items():
                if waited.get((ename, sn), 0) >= v:
                    continue
                eng.wait_ge(semobj[sn], v)
                waited[(ename, sn)] = v
            if op["fn"] is None:
                continue
            ins = op["fn"](eng)
            if op.get("cc"):
                cnt["cc"] += 1
                ins.then_inc(sems["cc"], 1)
                done[i] = ("c_cc", cnt["cc"])
            elif op["dma"]:
                ins.then_inc(semobj[sname], 16)
                done[i] = (sname, val)
            elif op["mark"]:
                cnt[ename] += 1
                ins.then_inc(sems[ename], 1)
                done[i] = (f"c_{ename}", cnt[ename])
        self.nmarks = dict(cnt)


def _uid(P):
    P._uidc = getattr(P, "_uidc", 0) + 1
    return P._uidc


def _sb(P, name, shape, dt):
    return P.es.enter_context(P.nc.sbuf_tensor(f"s{_uid(P)}_{name}", shape, dt))


def _ps(P, name, shape, dt):
    return P.es.enter_context(P.nc.psum_tensor(f"p{_uid(P)}_{name}", shape, dt))


def _run(build_fn, in_maps):
    nc = bass.Bass("TRN2", target_bir_lowering=False)
    with ExitStack() as es:
        P = Prog(nc, es)
        build_fn(nc, P)
        P.finish()
    res = run_bass_kernel_spmd(nc, in_maps, core_ids=list(range(NCORES)))
    return res.results


MOD_BLK = 6


def build_mod(nc, P):
    cT = nc.dram_tensor("cT", [128, 16], F32, kind="ExternalInput").ap()
    wblk = nc.dram_tensor("wblk", [MOD_BLK, D, 512], F32, kind="ExternalInput").ap()
    bblk = nc.dram_tensor("bblk", [2, MOD_BLK * 512], F32, kind="ExternalInput").ap()
    out = nc.dram_tensor("modo", [2, MOD_BLK * 512], F32, kind="ExternalOutput").ap()
    s_raw = _sb(P, "s_raw", [128, 16], F32)
    s = _sb(P, "s_act", [128, 16], F32)
    bt = _sb(P, "bt", [2, MOD_BLK * 512], F32)
    ot = _sb(P, "ot", [2, MOD_BLK * 512], F32)
    wt = [_sb(P, f"wt{i}", [128, 8, 512], F32) for i in range(2)]
    pst = [_ps(P, f"ps{i}", [128, 512], F32) for i in range(2)]
    P.dma("sp", s_raw[:], cT, w=["s_raw"])
    P.dma("sp", bt[:], bblk, w=["bt"])
    P.act(lambda e: e.activation(out=s[:], in_=s_raw[:], func=AF.Silu), r=["s_raw"], w=["s"])
    for b in range(MOD_BLK):
        w_ = wt[b % 2]
        pp = pst[b % 2]
        P.dma("sp", w_[:], wblk[b].rearrange("(k p) n -> p k n", p=128), w=[f"wt{b % 2}"])
        for k in range(8):
            P.pe(lambda e, k=k, w_=w_, pp=pp: e.matmul(pp[0:2, :], s[:, 2 * k:2 * k + 2], w_[:, k, :],
                                                      start=(k == 0), stop=(k == 7)),
                 r=["s", f"wt{b % 2}"], w=[f"ps{b % 2}"])
        P.dve(lambda e, b=b, pp=pp: e.tensor_tensor(out=ot[:, b * 512:(b + 1) * 512], in0=pp[0:2, :],
                                                   in1=bt[:, b * 512:(b + 1) * 512], op=ALU.add),
              r=[f"ps{b % 2}", "bt"], w=[("ot", b)])
    P.dma("sp", out, ot[:], r=[("ot", b) for b in range(MOD_BLK)], is_output=True)


def run_mod(c, c_ctx, ada_w, ada_b):
    cc = np.stack([c.reshape(D), c_ctx.reshape(D)], axis=0)
    cT = np.ascontiguousarray(cc.reshape(2, 8, 128).transpose(2, 1, 0)).reshape(128, 16)
    wb = np.ascontiguousarray(ada_w.reshape(DEPTH, D, 12, 512).transpose(0, 2, 1, 3)).reshape(48, D, 512)
    bb = ada_b.reshape(48, 512)
    in_maps = []
    for ci in range(NCORES):
        sl = slice(ci * MOD_BLK, (ci + 1) * MOD_BLK)
        brow = bb[sl].reshape(1, MOD_BLK * 512)
        in_maps.append({"cT": cT, "wblk": np.ascontiguousarray(wb[sl]),
                        "bblk": np.ascontiguousarray(np.concatenate([brow, brow], axis=0))})
    res = _run(build_mod, in_maps)
    mod = np.concatenate([r["modo"].reshape(2, MOD_BLK, 512) for r in res], axis=1)
    return mod.reshape(2, DEPTH, 6 * D)


MIX = {
    "A": dict(ncols=3072, rope=True, nqp=8, nkp=8, vh=8, dv=128, koff=1024, kcols=1024, voff=2048),
    "B": dict(ncols=3072, rope=False, nqp=8, nkp=8, vh=16, dv=64, koff=1024, kcols=1024, voff=2048),
    "C": dict(ncols=1536, rope=True, nqp=8, nkp=4, vh=4, dv=64, koff=1024, kcols=256, voff=1280),
}
TOKW = NTT * 128


def _bc(ap, shape):
    return ap.broadcast_to(shape)


def emit_norm_T(P, xt, xkey, hT, hkey, Asc, Bsc, ident, bufs, tag):
    junk, ssq, lnv, rstd, xn, ptr = bufs
    import os
    SUB = int(os.environ.get("PRE_SUB", "9"))
    P.act(lambda e: e.activation(out=junk[:], in_=xt, func=AF.Square, accum_out=ssq[:, 0:1]),
          r=[xkey], w=[tag + "junk", tag + "ssq"])
    if SUB < 2:
        return
    P.act(lambda e: e.activation(out=lnv[:, 0:1], in_=ssq[:, 0:1], func=AF.Ln, scale=1.0 / D, bias=EPS),
          r=[tag + "ssq"], w=[tag + "lnv"])
    if SUB < 3:
        return
    P.act(lambda e: e.activation(out=rstd[:, 0:1], in_=lnv[:, 0:1], func=AF.Exp, scale=-0.5),
          r=[tag + "lnv"], w=[tag + "rstd"])
    P.dve(lambda e: e.tensor_scalar(out=xn[:], in0=xt, scalar1=rstd[:, 0:1], scalar2=None, op0=ALU.mult),
          r=[xkey, tag + "rstd"], w=[tag + "xn"])
    if SUB < 4:
        return
    for fc in range(8):
        P.pe(lambda e, fc=fc: e.transpose(ptr[:, fc * 128:(fc + 1) * 128], xn[:, fc * 128:(fc + 1) * 128], ident[:]),
             r=[tag + "xn", "ident"], w=["PS:" + tag + "ptr"])
    if SUB < 5:
        return
    EV = os.environ.get("EVAC", "dve")
    for fc in range(8):
        if (fc % 2 == 0 and EV == "mixed") or EV == "dve":
            P.dve(lambda e, fc=fc: e.tensor_scalar(out=hT[:, fc, :], in0=ptr[:, fc * 128:(fc + 1) * 128],
                                                  scalar1=Asc[:, fc:fc + 1], scalar2=Bsc[:, fc:fc + 1],
                                                  op0=ALU.mult, op1=ALU.add),
                  r=["PS:" + tag + "ptr", "modsc"], w=[(hkey, fc)])
        else:
            P.act(lambda e, fc=fc: e.activation(out=hT[:, fc, :], in_=ptr[:, fc * 128:(fc + 1) * 128],
                                               func=AF.Identity, scale=Asc[:, fc:fc + 1], bias=Bsc[:, fc:fc + 1]),
                  r=["PS:" + tag + "ptr", "modsc"], w=[(hkey, fc)])


def emit_modsc(P, modT_d, gT_d, which):
    mt = _sb(P, "modT", [128, 8, 8], F32)
    gt = _sb(P, "gT", [128, 4, 8], F32)
    A = _sb(P, "modA", [128, 2, 8], F32)
    P.dma("sp", mt[:], modT_d, w=["modT_raw"])
    P.dma("sp", gt[:], gT_d, w=["gT_raw"])
    sh, sc = 2 * which, 2 * which + 1
    for ci in range(2):
        P.dve(lambda e, ci=ci: e.tensor_scalar(out=A[:, ci, :], in0=mt[:, 4 * ci + sc, :], scalar1=1.0,
                                              scalar2=None, op0=ALU.add),
              r=["modT_raw"], w=[("modA0", ci)])
        P.dve(lambda e, ci=ci: e.tensor_tensor(out=A[:, ci, :], in0=A[:, ci, :], in1=gt[:, 2 * which, :],
                                              op=ALU.mult),
              r=[("modA0", ci), "gT_raw"], w=["modsc"] if ci == 1 else [("modA1", ci)])
    return A[:, 0, :], mt[:, sh, :], A[:, 1, :], mt[:, 4 + sh, :]


def build_pre(mix):
    cfg = MIX[mix]
    ncols, nqp, nkp, vh, dv = cfg["ncols"], cfg["nqp"], cfg["nkp"], cfg["vh"], cfg["dv"]
    vw = vh * (dv + 1)

    def build(nc, P):
        x_d = nc.dram_tensor("x", [TOKW, D], F32, kind="ExternalInput").ap()
        w_d = nc.dram_tensor("wqkv", [D, ncols], F32, kind="ExternalInput").ap()
        modT_d = nc.dram_tensor("modT", [128, 8, 8], F32, kind="ExternalInput").ap()
        gT_d = nc.dram_tensor("gT", [128, 4, 8], F32, kind="ExternalInput").ap()
        id_d = nc.dram_tensor("ident", [128, 128], F32, kind="ExternalInput").ap()
        rc_d = nc.dram_tensor("ropeC", [TPC, 64], F32, kind="ExternalInput").ap()
        rs_d = nc.dram_tensor("ropeS", [TPC, 64], F32, kind="ExternalInput").ap()
        qT_d = nc.dram_tensor("qT", [nqp, 128, TOKW], BF16, kind="ExternalOutput").ap()
        kT_d = nc.dram_tensor("kT", [nkp, 128, TOKW], BF16, kind="ExternalOutput").ap()
        v_d = nc.dram_tensor("v", [vh, 128, NTT, dv + 1], BF16, kind="ExternalOutput").ap()
        emit_pre(nc, P, mix, x_d, w_d, modT_d, gT_d, id_d, rc_d, rs_d, qT_d, kT_d, v_d)
    return build


def emit_pre(nc, P, mix, x_d, w_d, modT_d, gT_d, id_d, rc_d, rs_d, qT_d, kT_d, v_d):
    cfg = MIX[mix]
    ncols, nqp, nkp, vh, dv = cfg["ncols"], cfg["nqp"], cfg["nkp"], cfg["vh"], cfg["dv"]
    koff, kcols, voff = cfg["koff"], cfg["kcols"], cfg["voff"]
    nch = ncols // 512
    ident = _sb(P, "ident", [128, 128], BF16)
    P.dma("pool", ident[:], id_d, w=["ident"])
    W = _sb(P, "W", [128, 8, ncols], BF16)
    for k in range(8):
        P.dma("pool", W[:, k, :], w_d[k * 128:(k + 1) * 128, :], w=[("W", k)])
    Wkeys = [("W", k) for k in range(8)]
    A_l, B_l, A_c, B_c = emit_modsc(P, modT_d, gT_d, 0)
    ropeC = _sb(P, "ropeC", [128, NLT, 64], F32)
    ropeS = _sb(P, "ropeS", [128, NLT, 64], F32)
    P.dma("sp", ropeC[:], rc_d.rearrange("(t p) j -> p t j", p=128), w=["ropeC"])
    P.dma("sp", ropeS[:], rs_d.rearrange("(t p) j -> p t j", p=128), w=["ropeS"])
    xb = [_sb(P, f"xb{i}", [128, D], F32) for i in range(2)]
    junk = _sb(P, "junk", [128, D], BF16)
    ssq = _sb(P, "ssq", [128, 1], F32)
    lnv = _sb(P, "lnv", [128, 1], F32)
    rstd = _sb(P, "rstd", [128, 1], F32)
    xn = _sb(P, "xn", [128, D], BF16)
    ptr = _ps(P, "ptr", [128, D], BF16)
    hT = [_sb(P, f"hT{i}", [128, 8, 128], BF16) for i in range(2)]
    pq = [_ps(P, f"pq{i}", [128, 512], F32) for i in range(3)]
    t1 = _sb(P, "t1", [128, 512], F32)
    t2 = _sb(P, "t2", [128, 512], F32)
    qk = _sb(P, "qk", [128, 2048], BF16)
    ptr2 = [_ps(P, f"ptr2{i}", [128, 1024], BF16) for i in range(2)]
    GT = 4
    qTg = _sb(P, "qTg", [128, nqp, GT * 128], BF16)
    kTg = _sb(P, "kTg", [128, nkp, GT * 128], BF16)
    vt = [_sb(P, f"vt{i}", [128, vh, dv + 1], BF16) for i in range(2)]
    for i in range(2):
        P.dve(lambda e, i=i: e.memset(vt[i][:], 1.0), w=[f"vt{i}"])
    nkt = nkp
    import os
    STG = int(os.environ.get("PRE_STAGE", "9"))
    for t in range(NTT if STG >= 9 else 1):
        is_ctx = t >= NLT
        xt = xb[t % 2]
        xkey = f"xb{t % 2}"
        P.dma("sp", xt[:], x_d[t * 128:(t + 1) * 128, :], w=[xkey])
        h = hT[t % 2]
        hkey = f"hT{t % 2}"
        if STG < 2:
            break
        emit_norm_T(P, xt[:], xkey, h, hkey, A_c if is_ctx else A_l, B_c if is_ctx else B_l, ident,
                    (junk, ssq, lnv, rstd, xn, ptr), "n")
        hkeys = [(hkey, fc) for fc in range(8)]
        g = t % GT
        if STG < 3:
            break
        for c in range(nch):
            pp = pq[c % 3]
            pkey = f"PS:pq{c % 3}"
            for k in range(8):
                P.pe(lambda e, k=k, pp=pp, c=c, h=h: e.matmul(pp[:], h[:, k, :], W[:, k, c * 512:(c + 1) * 512],
                                                             start=(k == 0), stop=(k == 7)),
                     r=hkeys + Wkeys, w=[pkey])
            c0 = c * 512
            if c0 >= voff:
                nh = 512 // dv
                h0 = (c0 - voff) // dv
                vv = vt[t % 2]
                P.act(lambda e, pp=pp, vv=vv, h0=h0, nh=nh: e.activation(
                    out=vv[:, h0:h0 + nh, 0:dv], in_=pp[:].rearrange("p (h d) -> p h d", d=dv), func=AF.Copy),
                    r=[pkey], w=[f"vt{t % 2}"])
                continue
            segs = []
            if mix == "C" and c == 2:
                segs.append(("k", 0, 256))
                segs.append(("v", 256, 256))
            else:
                segs.append(("q" if c0 < koff else "k", 0, 512))
            for kind, s0, sl in segs:
                if kind == "v":
                    vv = vt[t % 2]
                    P.act(lambda e, pp=pp, vv=vv, s0=s0, sl=sl: e.activation(
                        out=vv[:, :, 0:dv], in_=pp[:, s0:s0 + sl].rearrange("p (h d) -> p h d", d=dv),
                        func=AF.Copy), r=[pkey], w=[f"vt{t % 2}"])
                    continue
                nh = sl // 64
                src = pp[:, s0:s0 + sl]
                if kind == "q":
                    dsts = [qk[:, c0:c0 + sl]]
                elif mix == "C":
                    base = qk[:, 1024:1024 + 512].rearrange("p (h two d) -> p h two d", two=2, d=64)
                    dsts = [base[:, :, 0, :], base[:, :, 1, :]]
                else:
                    dsts = [qk[:, c0:c0 + sl]]
                if cfg["rope"] and not is_ctx:
                    x3 = src.rearrange("p (h d) -> p h d", d=64)
                    x5 = src.rearrange("p (h a b j) -> p h a b j", a=2, b=2, j=16)
                    Cb = _bc(ropeC[:, t, :].unsqueeze(1), [128, nh, 64])
                    S5 = ropeS[:, t, :].rearrange("p (a b j) -> p a b j", a=2, b=2, j=16)
                    t13 = t1[:, 0:sl].rearrange("p (h d) -> p h d", d=64)
                    t25 = t2[:, 0:sl].rearrange("p (h a b j) -> p h a b j", a=2, b=2, j=16)
                    P.dve(lambda e, x3=x3, Cb=Cb, t13=t13: e.tensor_tensor(out=t13, in0=x3, in1=Cb, op=ALU.mult),
                          r=[pkey, "ropeC"], w=["t1"])
                    for b in range(2):
                        Sb = _bc(S5[:, :, b, :].unsqueeze(1), [128, nh, 2, 16])
                        P.dve(lambda e, x5=x5, Sb=Sb, t25=t25, b=b: e.tensor_tensor(
                            out=t25[:, :, :, b, :], in0=x5[:, :, :, 1 - b, :], in1=Sb, op=ALU.mult),
                            r=[pkey, "ropeS"], w=[("t2", b)])
                    for dst in dsts:
                        if len(dsts) == 1:
                            P.dve(lambda e, dst=dst, sl=sl: e.tensor_tensor(out=dst, in0=t1[:, 0:sl], in1=t2[:, 0:sl],
                                                                         op=ALU.add),
                                  r=["t1", ("t2", 0), ("t2", 1)], w=["qk"])
                        else:
                            P.dve(lambda e, dst=dst, sl=sl: e.tensor_tensor(
                                out=dst, in0=t1[:, 0:sl].rearrange("p (h d) -> p h d", d=64),
                                in1=t2[:, 0:sl].rearrange("p (h d) -> p h d", d=64), op=ALU.add),
                                r=["t1", ("t2", 0), ("t2", 1)], w=["qk"])
                else:
                    for dst in dsts:
                        if len(dsts) == 1:
                            P.act(lambda e, dst=dst, src=src: e.activation(out=dst, in_=src, func=AF.Copy),
                                  r=[pkey], w=["qk"])
                        else:
                            P.act(lambda e, dst=dst, src=src: e.activation(
                                out=dst, in_=src.rearrange("p (h d) -> p h d", d=64), func=AF.Copy),
                                r=[pkey], w=["qk"])
        if STG < 4:
            break
        for j in range(8):
            P.pe(lambda e, j=j: e.transpose(ptr2[0][:, j * 128:(j + 1) * 128], qk[:, j * 128:(j + 1) * 128], ident[:]),
                 r=["qk", "ident"], w=["PS:ptr2_0"])
        for j in range(nkt):
            P.pe(lambda e, j=j: e.transpose(ptr2[1][:, j * 128:(j + 1) * 128],
                                            qk[:, 1024 + j * 128:1024 + (j + 1) * 128], ident[:]),
                 r=["qk", "ident"], w=["PS:ptr2_1"])
        P.act(lambda e, g=g: e.activation(out=qTg[:, :, g * 128:(g + 1) * 128],
                                          in_=ptr2[0][:].rearrange("p (a t) -> p a t", t=128), func=AF.Copy),
              r=["PS:ptr2_0"], w=[("qTg", g)])
        P.dve(lambda e, g=g: e.tensor_copy(out=kTg[:, :, g * 128:(g + 1) * 128],
                                           in_=ptr2[1][:, 0:nkt * 128].rearrange("p (a t) -> p a t", t=128)),
              r=["PS:ptr2_1"], w=[("kTg", g)])
        P.dma("sp", v_d[:, :, t, :].rearrange("h p d -> p h d"), vt[t % 2][:],
              r=[f"vt{t % 2}"], is_output=True)
        ng = GT if t < NLT else NCT
        if g == ng - 1:
            tok0 = (t - g) * 128
            P.dma("sp", qT_d[:, :, tok0:tok0 + ng * 128].rearrange("a p t -> p a t"), qTg[:, :, 0:ng * 128],
                  r=[("qTg", i) for i in range(ng)], is_output=True)
            P.dma("sp", kT_d[:, :, tok0:tok0 + ng * 128].rearrange("a p t -> p a t"), kTg[:, :, 0:ng * 128],
                  r=[("kTg", i) for i in range(ng)], is_output=True)


def _fm(v):
    return np.ascontiguousarray(v.reshape(8, 128).T)


def layer_consts(mod, norm_g, i):
    ml, mc = mod[0, i].reshape(6, D), mod[1, i].reshape(6, D)
    vecs = [ml[0], ml[1], ml[3], ml[4], mc[0], mc[1], mc[3], mc[4]]
    modT = np.ascontiguousarray(np.stack([_fm(v) for v in vecs], axis=1)).astype(np.float32)
    gT = np.ascontiguousarray(np.stack([_fm(norm_g[i, j]) for j in range(4)], axis=1)).astype(np.float32)
    rows = np.ascontiguousarray(np.stack([ml[2], ml[5], mc[2], mc[5], norm_g[i, 1], norm_g[i, 3]], axis=0))
    return modT, gT, rows.astype(np.float32)


def rope_tables():
    half = 32
    freqs = (10000.0 ** (-np.arange(0, half, 2, dtype=np.float32) / half)).astype(np.float32)
    pos = np.arange(SEQ)
    rows = (pos // GRID_W).astype(np.float32)
    cols = (pos % GRID_W).astype(np.float32)
    ar = rows[:, None] * freqs
    ac = cols[:, None] * freqs
    cr, sr, cc, sc = np.cos(ar), np.sin(ar), np.cos(ac), np.sin(ac)
    C = np.concatenate([cr, cr, cc, cc], axis=1).astype(np.float32)
    S = np.concatenate([-sr, sr, -sc, sc], axis=1).astype(np.float32)
    return C, S


def run_pre(mix, x_all, ctx, wqkv, modT, gT):
    C, S = rope_tables()
    ident = np.eye(128, dtype=np.float32)
    in_maps = []
    for ci in range(NCORES):
        xs = np.concatenate([x_all[ci * TPC:(ci + 1) * TPC], ctx], axis=0)
        in_maps.append({"x": np.ascontiguousarray(xs), "wqkv": wqkv, "modT": modT, "gT": gT, "ident": ident,
                        "ropeC": np.ascontiguousarray(C[ci * TPC:(ci + 1) * TPC]),
                        "ropeS": np.ascontiguousarray(S[ci * TPC:(ci + 1) * TPC])})
    return _run(build_pre(mix), in_maps)


NKA = CTX + SEQ
NKC_A = NKA // 128


def build_att_A(lam_init, limit=None):
    def build(nc, P):
        qT_d = nc.dram_tensor("qT", [8, 128, TOKW], BF16, kind="ExternalInput").ap()
        kT_d = nc.dram_tensor("kT", [8, 128, NKA], BF16, kind="ExternalInput").ap()
        v_d = nc.dram_tensor("v", [8, 128, NKC_A, 129], BF16, kind="ExternalInput").ap()
        lam_d = nc.dram_tensor("lam", [128, 256], F32, kind="ExternalInput").ap()
        sub_d = nc.dram_tensor("subln", [128, 128], F32, kind="ExternalInput").ap()
        on_d = nc.dram_tensor("on", [TOKW, D], BF16, kind="ExternalOutput").ap()
        emit_att_A(nc, P, lam_init, qT_d, kT_d, v_d, lam_d, sub_d, on_d, limit)
    return build


def emit_att_A(nc, P, lam_init, qT_d, kloc, vloc, kall, vall, lam_d, sub_d, on_d, limit=None):
    nheads, nqc_lim, nkc_lim = limit if limit else (8, 5, NKC_A)

    def piece_of(kc):
        return 0 if kc < 2 else 1 + (kc - 2) // 16
    lam = _sb(P, "lam", [128, 256], F32)
    sub = _sb(P, "sub", [128, 128], F32)
    wsub = _sb(P, "wsub", [128, 128], F32)
    lj = _sb(P, "lamjunk", [128, 64], F32)
    ls = _sb(P, "lams", [128, 2], F32)
    le = _sb(P, "lame", [128, 2], F32)
    neglam = _sb(P, "neglam", [128, 1], F32)
    P.dma("sp", lam[:], lam_d, w=["lam"])
    P.dma("sp", sub[:], sub_d, w=["sub"])
    P.dve(lambda e: e.tensor_scalar(out=wsub[:], in0=sub[:], scalar1=float(1.0 - lam_init), scalar2=None,
                                    op0=ALU.mult), r=["sub"], w=["wsub"])
    for i in range(2):
        P.dve(lambda e, i=i: e.scalar_tensor_tensor(out=lj[:], in0=lam[:, 128 * i:128 * i + 64], scalar=1.0,
                                                   in1=lam[:, 128 * i + 64:128 * i + 128], op0=ALU.mult,
                                                   op1=ALU.mult, accum_out=ls[:, i:i + 1]),
              r=["lam"], w=["lj", ("ls", i)])
    P.act(lambda e: e.activation(out=le[:], in_=ls[:], func=AF.Exp), r=[("ls", 0), ("ls", 1)], w=["le"])
    P.dve(lambda e: e.tensor_tensor(out=neglam[:], in0=le[:, 1:2], in1=le[:, 0:1], op=ALU.subtract),
          r=["le"], w=["neglam0"])
    P.dve(lambda e: e.tensor_scalar(out=neglam[:], in0=neglam[:], scalar1=float(-lam_init), scalar2=None,
                                    op0=ALU.add), r=["neglam0"], w=["neglam"])
    Kt = [_sb(P, f"Kt{i}", [128, NKA], BF16) for i in range(2)]
    Vt = [_sb(P, f"Vt{i}", [128, NKC_A, 129], BF16) for i in range(2)]
    Qt = [_sb(P, f"Qt{i}", [128, TOKW], BF16) for i in range(2)]
    NPB = 3
    Pb = [[_sb(P, f"Pb{b}_{m}", [128, 512], BF16) for m in range(2)] for b in range(NPB)]
    Sb = [[_ps(P, f"S{b}_{m}", [128, 512], F32) for m in range(2)] for b in range(2)]
    Ob = [_ps(P, f"O{i}", [128, 512], F32) for i in range(3)]
    o1 = _sb(P, "o1", [128, 4, 128], F32)
    o2 = _sb(P, "o2", [128, 4, 128], F32)
    fj = _sb(P, "fjunk", [128, 128], F32)
    rs = _sb(P, "rs", [128, 4, 2], F32)
    r1 = _sb(P, "r1", [128, 4], F32)
    ssq = _sb(P, "assq", [128, 4], F32)
    lnv = _sb(P, "alnv", [128, 4], F32)
    rstd = _sb(P, "arstd", [128, 4], F32)
    ON = [_sb(P, f"ON{i}", [128, 4, 128], BF16) for i in range(2)]

    def oslot(m, j):
        s = m * 4 + j
        return s // 3, (s % 3) * 129

    nfin = 0
    for h in range(nheads):
        kt, vt, qt = Kt[h % 2], Vt[h % 2], Qt[h % 2]
        kk, vk, qk_ = f"Kt{h % 2}", f"Vt{h % 2}", f"Qt{h % 2}"
        P.dma("sp", kt[:, 0:CTX], kloc[h, :, TPC:TOKW], w=[(kk, 0)])
        P.dma("sp", vt[:, 0:NCT, :], vloc[h, :, NLT:NTT, :], w=[(vk, 0)])
        for rk_ in range(NCORES):
            P.dma("sp", kt[:, CTX + rk_ * TPC:CTX + (rk_ + 1) * TPC], kall[rk_, h * 128:(h + 1) * 128, 0:TPC],
                  w=[(kk, 1 + rk_)])
            P.dma("sp", vt[:, NCT + rk_ * NLT:NCT + (rk_ + 1) * NLT, :],
                  vall[rk_, h * 128:(h + 1) * 128, 0:NLT * 129].rearrange("p (c d) -> p c d", d=129),
                  w=[(vk, 1 + rk_)])
        P.dma("sp", qt[:], qT_d[h], w=[qk_])
        for qc in range(nqc_lim):
            is_ctx = qc == 4
            q0 = qc * 512
            nq = 256 if is_ctx else 512
            nj = nq // 128
            chunks = [0, 1] if is_ctx else list(range(min(NKC_A, nkc_lim)))
            nch = len(chunks)

            def stage_S(i):
                kc = chunks[i]
                for m in range(2):
                    sb_ = Sb[i % 2][m]
                    P.pe(lambda e, m=m, kc=kc, sb_=sb_, kt=kt, qt=qt, q0=q0, nq=nq: e.matmul(
                        sb_[:, 0:nq], kt[m * 64:(m + 1) * 64, kc * 128:(kc + 1) * 128],
                        qt[m * 64:(m + 1) * 64, q0:q0 + nq], start=True, stop=True),
                        r=[(kk, piece_of(kc)), qk_], w=[f"PS:S{i % 2}_{m}"])

            def stage_E(i):
                for m in range(2):
                    sb_ = Sb[i % 2][m]
                    pb_ = Pb[i % NPB][m]
                    P.act(lambda e, sb_=sb_, pb_=pb_, nq=nq: e.activation(out=pb_[:, 0:nq], in_=sb_[:, 0:nq], func=AF.Exp,
                                                                  scale=0.125),
                          r=[f"PS:S{i % 2}_{m}"], w=[f"Pb{i % NPB}_{m}"])

            started = set()

            def stage_V(i, started=started):
                kc = chunks[i]
                for m in range(2):
                    pb_ = Pb[i % NPB][m]
                    for j in range(nj):
                        bank, off = oslot(m, j)
                        first_in_bank = (i == 0) and (bank not in started)
                        started.add(bank)
                        P.pe(lambda e, pb_=pb_, j=j, bank=bank, off=off, kc=kc, fb=first_in_bank, i=i, vt=vt, nch=nch: e.matmul(
                            Ob[bank][:, off:off + 129], pb_[:, j * 128:(j + 1) * 128], vt[:, kc, :],
                            start=fb, stop=(i == nch - 1), skip_group_check=True),
                            r=[f"Pb{i % NPB}_{m}", (vk, piece_of(kc))], w=[f"PS:O{bank}"])

            stage_S(0)
            if nch > 1:
                stage_S(1)
            stage_E(0)
            for i in range(nch):
                stage_V(i)
                if i + 2 < nch:
                    stage_S(i + 2)
                if i + 1 < nch:
                    stage_E(i + 1)
            on = ON[nfin % 2]
            onk = f"ON{nfin % 2}"
            nfin += 1
            for j in range(nj):
                b0, f0 = oslot(0, j)
                b1, f1 = oslot(1, j)
                P.dve(lambda e, j=j, b0=b0, f0=f0: e.reciprocal(out=rs[:, j, 0:1], in_=Ob[b0][:, f0 + 128:f0 + 129]),
                      r=[f"PS:O{b0}"], w=[("rs0", j)])
                P.dve(lambda e, j=j, b1=b1, f1=f1: e.reciprocal(out=rs[:, j, 1:2], in_=Ob[b1][:, f1 + 128:f1 + 129]),
                      r=[f"PS:O{b1}"], w=[("rs1", j)])
                P.dve(lambda e, j=j: e.tensor_tensor(out=r1[:, j:j + 1], in0=rs[:, j, 1:2], in1=neglam[:, 0:1],
                                                     op=ALU.mult), r=[("rs1", j), "neglam"], w=[("r1", j)])
                P.dve(lambda e, j=j, b0=b0, f0=f0: e.tensor_scalar(out=o1[:, j, :], in0=Ob[b0][:, f0:f0 + 128],
                                                                  scalar1=rs[:, j, 0:1], scalar2=None, op0=ALU.mult),
                      r=[f"PS:O{b0}", ("rs0", j)], w=[("o1", j)])
                P.dve(lambda e, j=j, b1=b1, f1=f1: e.scalar_tensor_tensor(
                    out=o2[:, j, :], in0=Ob[b1][:, f1:f1 + 128], scalar=r1[:, j:j + 1], in1=o1[:, j, :],
                    op0=ALU.mult, op1=ALU.add), r=[f"PS:O{b1}", ("r1", j), ("o1", j)], w=[("o2", j)])
            for j in range(nj):
                P.dve(lambda e, j=j: e.scalar_tensor_tensor(out=fj[:], in0=o2[:, j, :], scalar=1.0, in1=o2[:, j, :],
                                                            op0=ALU.mult, op1=ALU.mult, accum_out=ssq[:, j:j + 1]),
                      r=[("o2", j)], w=["fj", ("assq", j)])
            P.act(lambda e, nj=nj: e.activation(out=lnv[:, 0:nj], in_=ssq[:, 0:nj], func=AF.Ln, scale=1.0 / 128, bias=EPS),
                  r=[("assq", j) for j in range(nj)], w=["alnv"])
            P.act(lambda e, nj=nj: e.activation(out=rstd[:, 0:nj], in_=lnv[:, 0:nj], func=AF.Exp, scale=-0.5),
                  r=["alnv"], w=["arstd"])
            for j in range(nj):
                P.dve(lambda e, j=j, on=on: e.scalar_tensor_tensor(out=on[:, j, :], in0=o2[:, j, :],
                                                                  scalar=rstd[:, j:j + 1], in1=wsub[:],
                                                                  op0=ALU.mult, op1=ALU.mult),
                      r=[("o2", j), "arstd", "wsub"], w=[(onk, j)])
            P.dma("sp", on_d[q0:q0 + nq, h * 128:(h + 1) * 128].rearrange("(j p) d -> p j d", p=128),
                  on[:, 0:nj, :], r=[(onk, j) for j in range(nj)], is_output=True)


def gather_kv_A(pre_res):
    kT = np.concatenate([np.asarray(pre_res[0]["kT"])[:, :, TPC:]] +
                        [np.asarray(r["kT"])[:, :, :TPC] for r in pre_res], axis=2)
    v = np.concatenate([np.asarray(pre_res[0]["v"])[TPC:]] + [np.asarray(r["v"])[:TPC] for r in pre_res], axis=0)
    v = v.reshape(NKC_A, 128, 8, 129).transpose(2, 1, 0, 3)
    return np.ascontiguousarray(kT), np.ascontiguousarray(v)


def run_att_A(pre_res, a_lambda_j, a_subln_j, lam_init):
    kT, v = gather_kv_A(pre_res)
    lam = np.ascontiguousarray(np.broadcast_to(a_lambda_j.reshape(1, 256), (128, 256))).astype(np.float32)
    sub = np.ascontiguousarray(np.broadcast_to(a_subln_j.reshape(1, 128), (128, 128))).astype(np.float32)
    in_maps = [{"qT": np.asarray(pre_res[ci]["qT"]), "kT": kT, "v": v, "lam": lam, "subln": sub}
               for ci in range(NCORES)]
    return _run(build_att_A(lam_init), in_maps)


def emit_resid(P, yb, ykeys, xt, xkey, G, gkey, bufs, tag):
    junk, ssq2, ssq, lnv, rstd, tmp = bufs
    for c in range(2):
        P.act(lambda e, c=c: e.activation(out=junk[:, 0:512], in_=yb[c][:], func=AF.Square,
                                          accum_out=ssq2[:, c:c + 1]),
              r=[ykeys[c]], w=[tag + "junk", (tag + "ssq2", c)])
    P.dve(lambda e: e.tensor_tensor(out=ssq[:, 0:1], in0=ssq2[:, 0:1], in1=ssq2[:, 1:2], op=ALU.add),
          r=[(tag + "ssq2", 0), (tag + "ssq2", 1)], w=[tag + "ssq"])
    P.act(lambda e: e.activation(out=lnv[:, 0:1], in_=ssq[:, 0:1], func=AF.Ln, scale=1.0 / D, bias=EPS),
          r=[tag + "ssq"], w=[tag + "lnv"])
    P.act(lambda e: e.activation(out=rstd[:, 0:1], in_=lnv[:, 0:1], func=AF.Exp, scale=-0.5),
          r=[tag + "lnv"], w=[tag + "rstd"])
    for c in range(2):
        P.dve(lambda e, c=c: e.scalar_tensor_tensor(out=tmp[:, c * 512:(c + 1) * 512], in0=yb[c][:],
                                                    scalar=rstd[:, 0:1], in1=G[:, c * 512:(c + 1) * 512],
                                                    op0=ALU.mult, op1=ALU.mult),
              r=[ykeys[c], tag + "rstd", gkey], w=[(tag + "tmp", c)])
    P.pool(lambda e: e.tensor_tensor(out=xt, in0=xt, in1=tmp[:], op=ALU.add),
           r=[(tag + "tmp", 0), (tag + "tmp", 1), xkey], w=[xkey])


def emit_gates(P, rows_d, which):
    gr = _sb(P, "gr", [128, D], F32)
    G = _sb(P, "G", [128, 2, D], F32)
    P.dma("sp", gr[:], rows_d[4 + which:5 + which, :].partition_broadcast(128), w=["gr"])
    for i in range(2):
        s = which + 2 * i
        P.dma("sp", G[:, i, :], rows_d[s:s + 1, :].partition_broadcast(128), w=[("G0", i)])
        P.dve(lambda e, i=i: e.tensor_tensor(out=G[:, i, :], in0=G[:, i, :], in1=gr[:], op=ALU.mult),
              r=[("G0", i), "gr"], w=[("G", i)])
    return G


def build_post():
    def build(nc, P):
        x_d = nc.dram_tensor("x", [TOKW, D], F32, kind="ExternalInput").ap()
        on_d = nc.dram_tensor("on", [TOKW, D], BF16, kind="ExternalInput").ap()
        wo_d = nc.dram_tensor("wo", [D, D], F32, kind="ExternalInput").ap()
        w13_d = nc.dram_tensor("w13", [D, 2 * FFN], F32, kind="ExternalInput").ap()
        w2_d = nc.dram_tensor("w2", [FFN, D], F32, kind="ExternalInput").ap()
        modT_d = nc.dram_tensor("modT", [128, 8, 8], F32, kind="ExternalInput").ap()
        gT_d = nc.dram_tensor("gT", [128, 4, 8], F32, kind="ExternalInput").ap()
        rows_d = nc.dram_tensor("rows", [6, D], F32, kind="ExternalInput").ap()
        id_d = nc.dram_tensor("ident", [128, 128], F32, kind="ExternalInput").ap()
        xm_d = nc.dram_tensor("xmid", [TOKW, D], F32, kind="Internal").ap()
        xo_d = nc.dram_tensor("xo", [TOKW, D], F32, kind="ExternalOutput").ap()
        emit_post(nc, P, x_d, on_d, wo_d, w13_d, w2_d, modT_d, gT_d, rows_d, id_d, xm_d, xo_d)
    return build


def emit_post(nc, P, x_d, on_d, wo_d, w13_d, w2_d, modT_d, gT_d, rows_d, id_d, xm_d, xo_d):
    outer = P.es
    with ExitStack() as es1:
        P.es = es1
        ident = _sb(P, "ident1", [128, 128], BF16)
        P.dma("pool", ident[:], id_d, w=["ident"])
        Wo = _sb(P, "Wo", [128, 8, D], BF16)
        for k in range(8):
            P.dma("pool", Wo[:, k, :], wo_d[k * 128:(k + 1) * 128, :], w=[("Wo", k)])
        Wok = [("Wo", k) for k in range(8)]
        G = emit_gates(P, rows_d, 0)
        xb = [_sb(P, f"pxb{i}", [128, D], F32) for i in range(2)]
        onb = [_sb(P, f"onb{i}", [128, D], BF16) for i in range(2)]
        onT = [_sb(P, f"onT{i}", [128, 8, 128], BF16) for i in range(2)]
        ptr = _ps(P, "p1tr", [128, D], BF16)
        yb = [[_ps(P, f"p1y{i}_{c}", [128, 512], F32) for c in range(2)] for i in range(2)]
        bufs = (_sb(P, "p1junk", [128, 512], BF16), _sb(P, "p1ssq2", [128, 2], F32), _sb(P, "p1ssq", [128, 1], F32),
                _sb(P, "p1lnv", [128, 1], F32), _sb(P, "p1rstd", [128, 1], F32), _sb(P, "p1tmp", [128, D], F32))
        for t in range(NTT):
            is_ctx = t >= NLT
            xt, xkey = xb[t % 2], f"pxb{t % 2}"
            ot, okey = onb[t % 2], f"onb{t % 2}"
            oT, oTkey = onT[t % 2], f"onT{t % 2}"
            P.dma("sp", xt[:], x_d[t * 128:(t + 1) * 128, :], w=[xkey])
            P.dma("sp", ot[:], on_d[t * 128:(t + 1) * 128, :], w=[okey])
            for fc in range(8):
                P.pe(lambda e, fc=fc, ot=ot: e.transpose(ptr[:, fc * 128:(fc + 1) * 128],
                                                        ot[:, fc * 128:(fc + 1) * 128], ident[:]),
                     r=[okey, "ident"], w=["PS:p1tr"])
            P.dve(lambda e, oT=oT: e.tensor_copy(out=oT[:].rearrange("p a t -> p (a t)"), in_=ptr[:]),
                  r=["PS:p1tr"], w=[oTkey])
            ybt = yb[t % 2]
            ykeys = [f"PS:p1y{t % 2}_{c}" for c in range(2)]
            for c in range(2):
                for k in range(8):
                    P.pe(lambda e, c=c, k=k, oT=oT, ybt=ybt: e.matmul(ybt[c][:], oT[:, k, :],
                                                                    Wo[:, k, c * 512:(c + 1) * 512],
                                                                    start=(k == 0), stop=(k == 7)),
                         r=[oTkey] + Wok, w=[ykeys[c]])
            gi = 1 if is_ctx else 0
            emit_resid(P, ybt, ykeys, xt[:], xkey, G[:, gi, :], ("G", gi), bufs, "p1")
            P.dma("sp", xm_d[t * 128:(t + 1) * 128, :], xt[:], r=[xkey], w=[("xm", t)])
        P.barrier()
        P.emit()
    with ExitStack() as es2:
        P.es = es2
        ident = _sb(P, "ident2", [128, 128], BF16)
        P.dma("pool", ident[:], id_d, w=["ident"])
        NF = FFN // 128
        W13 = _sb(P, "W13", [128, 8, 2 * FFN], BF16)
        W2 = _sb(P, "W2", [128, NF, D], BF16)
        for k in range(8):
            for hh in range(2):
                P.dma("pool", W13[:, k, hh * FFN:(hh + 1) * FFN], w13_d[k * 128:(k + 1) * 128, hh * FFN:(hh + 1) * FFN],
                      w=[("W13", k, hh)])
        W13k = [("W13", k, hh) for k in range(8) for hh in range(2)]
        for fc in range(NF):
            P.dma("pool", W2[:, fc, :], w2_d[fc * 128:(fc + 1) * 128, :], w=[("W2", fc)])
        W2k = [("W2", fc) for fc in range(NF)]
        A_l, B_l, A_c, B_c = emit_modsc(P, modT_d, gT_d, 1)
        G = emit_gates(P, rows_d, 1)
        GT = 4
        xg = _sb(P, "xg", [128, GT, D], F32)
        h2T = _sb(P, "h2T", [128, 8, GT * 128], BF16)
        hid = _sb(P, "hid", [128, NF, GT * 128], BF16)
        sg = [_sb(P, f"sg{i}", [128, GT * 128], F32) for i in range(2)]
        ptr = _ps(P, "p2tr", [128, D], BF16)
        gb = [[_ps(P, f"p2g{i}_{c}", [128, 512], F32) for c in range(2)] for i in range(2)]
        yb = [_ps(P, f"p2y{c}", [128, 512], F32) for c in range(2)]
        nb = (_sb(P, "p2njunk", [128, D], BF16), _sb(P, "p2nssq", [128, 1], F32), _sb(P, "p2nlnv", [128, 1], F32),
              _sb(P, "p2nrstd", [128, 1], F32), _sb(P, "p2xn", [128, D], BF16), ptr)
        bufs = (_sb(P, "p2junk", [128, 512], BF16), _sb(P, "p2ssq2", [128, 2], F32), _sb(P, "p2ssq", [128, 1], F32),
                _sb(P, "p2lnv", [128, 1], F32), _sb(P, "p2rstd", [128, 1], F32), _sb(P, "p2tmp", [128, D], F32))
        ngroups = NLT // GT + 1
        for gidx in range(ngroups):
            is_ctx = gidx == ngroups - 1
            ng = NCT if is_ctx else GT
            ntok = ng * 128
            t0 = gidx * GT
            for j in range(ng):
                t = t0 + j
                P.dma("sp", xg[:, j, :], xm_d[t * 128:(t + 1) * 128, :], r=[("xm", t)], w=[("xg", j)])
                hview = h2T[:, :, j * 128:(j + 1) * 128]
                emit_norm_T(P, xg[:, j, :], ("xg", j), hview, ("h2T", j), A_c if is_ctx else A_l,
                            B_c if is_ctx else B_l, ident, nb, "p2n")
            hkeys = [(("h2T", j), fc) for j in range(ng) for fc in range(8)]
            for fc in range(NF):
                gbt = gb[fc % 2]
                gk = [f"PS:p2g{fc % 2}_{c}" for c in range(2)]
                for c in range(2):
                    col0 = c * FFN + fc * 128
                    for k in range(8):
                        P.pe(lambda e, c=c, k=k, col0=col0, gbt=gbt, ntok=ntok: e.matmul(
                            gbt[c][:, 0:ntok], W13[:, k, col0:col0 + 128], h2T[:, k, 0:ntok],
                            start=(k == 0), stop=(k == 7)), r=hkeys + W13k, w=[gk[c]])
                sgt = sg[fc % 2]
                P.act(lambda e, gbt=gbt, sgt=sgt, ntok=ntok: e.activation(out=sgt[:, 0:ntok], in_=gbt[0][:, 0:ntok],
                                                                         func=AF.Silu),
                      r=[gk[0]], w=[f"sg{fc % 2}"])
                P.dve(lambda e, gbt=gbt, sgt=sgt, fc=fc, ntok=ntok: e.tensor_tensor(
                    out=hid[:, fc, 0:ntok], in0=gbt[1][:, 0:ntok], in1=sgt[:, 0:ntok], op=ALU.mult),
                    r=[gk[1], f"sg{fc % 2}"], w=[("hid", fc)])
            hidk = [("hid", fc) for fc in range(NF)]
            for j in range(ng):
                t = t0 + j
                ykeys = [f"PS:p2y{c}" for c in range(2)]
                for c in range(2):
                    for fc in range(NF):
                        P.pe(lambda e, c=c, fc=fc, j=j: e.matmul(yb[c][:], hid[:, fc, j * 128:(j + 1) * 128],
                                                               W2[:, fc, c * 512:(c + 1) * 512],
                                                               start=(fc == 0), stop=(fc == NF - 1)),
                             r=hidk + W2k, w=[ykeys[c]])
                gi = 1 if is_ctx else 0
                emit_resid(P, yb, ykeys, xg[:, j, :], ("xg", j), G[:, gi, :], ("G", gi), bufs, "p2")
                P.dma("sp", xo_d[t * 128:(t + 1) * 128, :], xg[:, j, :], r=[("xg", j)], is_output=True)
        P.barrier()
        P.emit()
    P.es = outer


def run_post(x_all, ctx, on_res, wo, w13, w2, modT, gT, rows):
    ident = np.eye(128, dtype=np.float32)
    in_maps = []
    for ci in range(NCORES):
        xs = np.concatenate([x_all[ci * TPC:(ci + 1) * TPC], ctx], axis=0)
        in_maps.append({"x": np.ascontiguousarray(xs), "on": np.asarray(on_res[ci]["on"]), "wo": wo, "w13": w13,
                        "w2": w2, "modT": modT, "gT": gT, "rows": rows, "ident": ident})
    res = _run(build_post(), in_maps)
    x_new = np.concatenate([r["xo"][:TPC] for r in res], axis=0)
    ctx_new = res[0]["xo"][TPC:]
    return x_new, ctx_new


def emit_local_units(P, units, tagp=""):
    Sb = [_ps(P, f"lS{i}", [128, 512], F32) for i in range(4)]
    Ob = [_ps(P, f"lO{i}", [128, 512], F32) for i in range(2)]
    NPB = 3
    Pb = [[_sb(P, f"lPb{i}_{e}", [128, 512], BF16) for e in range(2)] for i in range(NPB)]
    Pt = [_sb(P, f"lPt{i}", [128, 512], BF16) for i in range(2)]
    Tf = [_sb(P, f"lTf{i}", [128, 512], F32) for i in range(2)]
    flat = []
    for ui, u in enumerate(units):
        for ii, it in enumerate(u["items"]):
            cnt = [0, 0]
            pos = []
            for sub in it["subs"]:
                e_ = sub[4]
                pos.append(cnt[e_])
                cnt[e_] += 1
            it["pos"] = pos
            it["cnt"] = cnt
            flat.append((ui, ii, it))
    n = len(flat)

    def stage_S(i):
        ui, ii, it = flat[i]
        if ii == 0 and units[ui].get("pre") is not None:
            units[ui]["pre"]()
        for si, (kap, qap, vap, og, e_) in enumerate(it["subs"]):
            bk = 2 * (i % 2) + e_
            sb_ = Sb[bk]
            ps_ = it["pos"][si]
            P.pe(lambda e, sb_=sb_, ps_=ps_, kap=kap, qap=qap: e.matmul(sb_[:, ps_ * 128:(ps_ + 1) * 128], kap, qap,
                                                                     start=True, stop=True, skip_group_check=True),
                 r=it["rk"], w=[f"PS:lS{bk}"])

    def stage_E(i):
        ui, ii, it = flat[i]
        for e_ in range(2):
            if it["cnt"][e_] == 0:
                continue
            bk = 2 * (i % 2) + e_
            sb_ = Sb[bk]
            pb_ = Pb[i % NPB][e_]
            w_ = it["cnt"][e_] * 128
            skey = f"PS:lS{bk}"
            pkey = f"lPb{i % NPB}_{e_}"
            if it["kind"] == "plain":
                P.act(lambda e, sb_=sb_, pb_=pb_, w_=w_: e.activation(out=pb_[:, 0:w_], in_=sb_[:, 0:w_], func=AF.Exp,
                                                                     scale=0.125), r=[skey], w=[pkey])
            elif it["kind"] == "mask":
                pt_ = Pt[e_]
                aux = it["aux"][e_]
                P.act(lambda e, sb_=sb_, pt_=pt_, w_=w_: e.activation(out=pt_[:, 0:w_], in_=sb_[:, 0:w_], func=AF.Exp,
                                                                     scale=0.125), r=[skey], w=[f"lPt{e_}"])
                P.dve(lambda e, pb_=pb_, pt_=pt_, aux=aux, w_=w_: e.tensor_tensor(
                    out=pb_[:, 0:w_].rearrange("p (g q) -> p g q", q=128),
                    in0=pt_[:, 0:w_].rearrange("p (g q) -> p g q", q=128), in1=aux, op=ALU.mult),
                    r=[f"lPt{e_}"] + it["auxk"], w=[pkey])
            else:
                tf_ = Tf[i % 2]
                aux = it["aux"]
                P.dve(lambda e, sb_=sb_, tf_=tf_, aux=aux, w_=w_: e.scalar_tensor_tensor(
                    out=tf_[:, 0:w_], in0=sb_[:, 0:w_], scalar=0.125, in1=aux, op0=ALU.mult, op1=ALU.add),
                    r=[skey] + it["auxk"], w=[f"lTf{i % 2}"])
                P.act(lambda e, tf_=tf_, pb_=pb_, w_=w_: e.activation(out=pb_[:, 0:w_], in_=tf_[:, 0:w_], func=AF.Exp),
                      r=[f"lTf{i % 2}"], w=[pkey])

    started = {}

    def stage_V(i):
        ui, ii, it = flat[i]
        u = units[ui]
        ob_ = Ob[ui % 2]
        nit = len(u["items"])
        for si, (kap, qap, vap, og, e_) in enumerate(it["subs"]):
            pb_ = Pb[i % NPB][e_]
            ps_ = it["pos"][si]
            first = started.get(ui) is None
            started[ui] = True
            last = (ii == nit - 1) and (si == len(it["subs"]) - 1)
            P.pe(lambda e, pb_=pb_, ob_=ob_, ps_=ps_, vap=vap, og=og, first=first, last=last: e.matmul(
                ob_[:, og * 65:(og + 1) * 65], pb_[:, ps_ * 128:(ps_ + 1) * 128], vap, start=first, stop=last,
                skip_group_check=True), r=[f"lPb{i % NPB}_{e_}"] + it["vk"], w=[f"PS:lO{ui % 2}"])
        if ii == nit - 1:
            u["fin"](u, ob_, f"PS:lO{ui % 2}")

    if n == 0:
        return
    stage_S(0)
    if n > 1:
        stage_S(1)
    stage_E(0)
    for i in range(n):
        stage_V(i)
        if i + 2 < n:
            stage_S(i + 2)
        if i + 1 < n:
            stage_E(i + 1)


def make_fin(P, G, ONt, onkey_fn, col0_fn, esink=None):
    den = _sb(P, "fden", [128, 4], F32)
    rr = _sb(P, "frr", [128, 4], F32)

    def fin(u, ob_, okey):
        ov = ob_[:, 0:G * 65].rearrange("p (g d) -> p g d", d=65)
        col0 = col0_fn(u)
        dst = ONt[:, u["blk"], col0:col0 + G * 64].rearrange("p (g d) -> p g d", d=64)
        if esink is not None:
            h0 = u["h0"]
            P.dve(lambda e: e.tensor_tensor(out=den[:, 0:G], in0=ov[:, :, 64], in1=esink[:, h0:h0 + G], op=ALU.add),
                  r=[okey, "esink"], w=["fden"])
            P.dve(lambda e: e.reciprocal(out=rr[:, 0:G], in_=den[:, 0:G]), r=["fden"], w=["frr"])
        else:
            P.dve(lambda e: e.reciprocal(out=rr[:, 0:G], in_=ov[:, :, 64]), r=[okey], w=["frr"])
        P.dve(lambda e: e.tensor_tensor(out=dst, in0=ov[:, :, 0:64],
                                        in1=rr[:, 0:G].unsqueeze(2).broadcast_to([128, G, 64]), op=ALU.mult),
              r=[okey, "frr"], w=[onkey_fn(u)])
    return fin


NKL_C = CTX + 128 + TPC + 128
NKC_C = NKL_C // 128


def build_att_C():
    def build(nc, P):
        qT_d = nc.dram_tensor("qT", [8, 128, TOKW], BF16, kind="ExternalInput").ap()
        kT_d = nc.dram_tensor("kT", [4, 128, NKL_C], BF16, kind="ExternalInput").ap()
        v_d = nc.dram_tensor("v", [128, NKC_C, 4 * 65], BF16, kind="ExternalInput").ap()
        mk_d = nc.dram_tensor("masks", [128, 4, 128], F32, kind="ExternalInput").ap()
        sk_d = nc.dram_tensor("sink", [128, 16], F32, kind="ExternalInput").ap()
        on_d = nc.dram_tensor("on", [TOKW, D], BF16, kind="ExternalOutput").ap()
        emit_att_C(nc, P, qT_d, kT_d, v_d, mk_d, sk_d, on_d)
    return build


def emit_att_C(nc, P, qT_d, kT_d, v_d, mk_d, sk_d, on_d):
    Q = _sb(P, "cQ", [128, 8, TOKW], BF16)
    Kc = _sb(P, "cK", [128, 4, NKL_C], BF16)
    V = _sb(P, "cV", [128, NKC_C, 4 * 65], BF16)
    mk = _sb(P, "cmk", [128, 4, 128], BF16)
    sk = _sb(P, "csk", [128, 16], F32)
    esink = _sb(P, "cesink", [128, 16], F32)
    ONt = _sb(P, "cON", [128, NTT, D], BF16)
    for p in range(8):
        P.dma("sp", Q[:, p, :], qT_d[p], w=[("cQ", p)])
    for kv in range(4):
        P.dma("sp", Kc[:, kv, :], kT_d[kv], w=[("cK", kv)])
    for kv in range(4):
        P.dma("sp", V[:, :, kv * 65:(kv + 1) * 65], v_d[kv], w=[("cV", kv)])
    P.dma("pool", mk[:], mk_d, w=["cmk"])
    P.dma("sp", sk[:], sk_d, w=["csk"])
    P.act(lambda e: e.activation(out=esink[:], in_=sk[:], func=AF.Exp), r=["csk"], w=["esink"])
    fin = make_fin(P, 4, ONt, lambda u: ("cON", u["blk"], u["kv"]), lambda u: u["kv"] * 256, esink)
    units = []
    for blk in range(NTT):
        is_ctx = blk >= NLT
        for kv in range(4):
            q0 = blk * 128
            chunks = [(0, "plain", None), (1, "plain", None)]
            if not is_ctx:
                chunks.append((2 + blk, "mask", 2 if blk == 0 else 0))
                chunks.append((3 + blk, "plain", None))
                chunks.append((4 + blk, "mask", 3 if blk == NLT - 1 else 1))
            items = []
            for (lc, kind, mi) in chunks:
                subs = []
                for g in range(4):
                    e_ = g % 2
                    pr = 2 * kv + g // 2
                    subs.append((Kc[e_ * 64:(e_ + 1) * 64, kv, lc * 128:(lc + 1) * 128],
                                 Q[e_ * 64:(e_ + 1) * 64, pr, q0:q0 + 128],
                                 V[:, lc, kv * 65:(kv + 1) * 65], g, e_))
                it = dict(subs=subs, kind=kind, rk=[("cK", kv), ("cQ", 2 * kv), ("cQ", 2 * kv + 1)], vk=[("cV", kv)])
                if kind == "mask":
                    mb = mk[:, mi, :].unsqueeze(1).broadcast_to([128, 2, 128])
                    it["aux"] = [mb, mb]
                    it["auxk"] = ["cmk"]
                items.append(it)
            units.append(dict(items=items, G=4, fin=fin, blk=blk, kv=kv, h0=4 * kv))
    for u in units:
        for it in u["items"]:
            if it["kind"] == "mask":
                it["mask3"] = True
    emit_local_units(P, units)
    for blk in range(NTT):
        P.dma("sp", on_d[blk * 128:(blk + 1) * 128, :], ONt[:, blk, :],
              r=[("cON", blk, kv) for kv in range(4)], is_output=True)


def run_att_C(pre_res, c_sink_j):
    kT_all = np.concatenate([np.asarray(r["kT"])[:, :, :TPC] for r in pre_res], axis=2)
    kT_ctx = np.asarray(pre_res[0]["kT"])[:, :, TPC:]
    v_all = np.concatenate([np.asarray(r["v"])[:TPC] for r in pre_res], axis=0)
    v_ctx = np.asarray(pre_res[0]["v"])[TPC:]
    zk = np.zeros((4, 128, 128), dtype=kT_all.dtype)
    zv = np.zeros((128, v_all.shape[1]), dtype=v_all.dtype)
    p = np.arange(128)
    m_lo = (p[:, None] >= p[None, :]).astype(np.float32)
    m_hi = (p[:, None] <= p[None, :]).astype(np.float32)
    sink = np.ascontiguousarray(np.broadcast_to(c_sink_j.reshape(1, 16), (128, 16))).astype(np.float32)
    in_maps = []
    for ci in range(NCORES):
        lo, hi = ci * TPC, (ci + 1) * TPC
        kb = kT_all[:, :, lo - 128:lo] if ci > 0 else zk
        ka = kT_all[:, :, hi:hi + 128] if ci < NCORES - 1 else zk
        vb = v_all[lo - 128:lo] if ci > 0 else zv
        va = v_all[hi:hi + 128] if ci < NCORES - 1 else zv
        kT = np.concatenate([kT_ctx, kb, kT_all[:, :, lo:hi], ka], axis=2)
        v = np.concatenate([v_ctx, vb, v_all[lo:hi], va], axis=0)
        v = v.reshape(NKC_C, 128, v.shape[1]).transpose(1, 0, 2)
        masks = np.stack([m_lo, m_hi, m_lo if ci > 0 else np.zeros_like(m_lo),
                          m_hi if ci < NCORES - 1 else np.zeros_like(m_hi)], axis=1)
        in_maps.append({"qT": np.asarray(pre_res[ci]["qT"]), "kT": np.ascontiguousarray(kT),
                        "v": np.ascontiguousarray(v), "masks": np.ascontiguousarray(masks).astype(np.float32),
                        "sink": sink})
    return _run(build_att_C(), in_maps)


NKL_B = CTX + 256 + TPC + 256
NKC_B = NKL_B // 128
NA_CLASSES = [(2, -2, 5), (0, -2, 6), (1, -2, 5), (14, -2, 5), (15, -3, 6)]


def na_class(blk):
    return {0: 1, 1: 2, 14: 3, 15: 4}.get(blk, 0)


def na_tables(rpb, core):
    out = np.full((16, 128, 5, 6, 128), -30000.0, dtype=np.float32)
    q = np.arange(128)
    p = np.arange(128)
    for cl, (b, dstart, ns) in enumerate(NA_CLASSES):
        gb = 16 * core + b
        qr = 2 * gb + q // 64
        qc = q % 64
        kr0 = np.clip(qr - 4, 0, 256 - 8)
        kc0 = np.clip(qc - 8, 0, 64 - 16)
        for s in range(ns):
            gc = gb + dstart + s
            if gc < 0 or gc > 127:
                continue
            kr = 2 * gc + p // 64
            kc = p % 64
            valid = ((kr[:, None] >= kr0[None, :]) & (kr[:, None] < kr0[None, :] + 8) &
                     (kc[:, None] >= kc0[None, :]) & (kc[:, None] < kc0[None, :] + 16))
            dr = np.clip(kr[:, None] - qr[None, :] + 7, 0, 14)
            dc = np.clip(kc[:, None] - qc[None, :] + 15, 0, 30)
            vals = rpb[:, dr, dc]
            out[:, :, cl, s, :] = np.where(valid[None], vals, np.float32(-30000.0))
    return out


def build_att_B():
    def build(nc, P):
        qT_d = nc.dram_tensor("qT", [8, 128, TOKW], BF16, kind="ExternalInput").ap()
        kT_d = nc.dram_tensor("kT", [8, 128, NKL_B], BF16, kind="ExternalInput").ap()
        v_d = nc.dram_tensor("v", [16, 128, NKC_B, 65], BF16, kind="ExternalInput").ap()
        b_d = nc.dram_tensor("bias", [16, 128, 5 * 6 * 128], F32, kind="ExternalInput").ap()
        on_d = nc.dram_tensor("on", [TOKW, D], BF16, kind="ExternalOutput").ap()
        emit_att_B(nc, P, qT_d, kT_d, v_d, b_d, on_d)
    return build


def emit_att_B(nc, P, qT_d, kT_d, v_d, b_d, on_d):
    Q = [_sb(P, f"bQ{i}", [128, TOKW], BF16) for i in range(2)]
    Kb = [_sb(P, f"bK{i}", [128, NKL_B], BF16) for i in range(2)]
    V = [_sb(P, f"bV{i}", [128, NKC_B, 65], BF16) for i in range(2)]
    Bt = [_sb(P, f"bB{i}", [128, 5, 6, 128], F32) for i in range(2)]
    ONt = _sb(P, "bON", [128, NTT, D], BF16)

    def load_head(h):
        pr, e_ = h // 2, h % 2
        if e_ == 0:
            P.dma("sp", Q[pr % 2][:], qT_d[pr], w=[f"bQ{pr % 2}"])
            P.dma("sp", Kb[pr % 2][:], kT_d[pr], w=[f"bK{pr % 2}"])
        P.dma("sp", V[h % 2][:], v_d[h], w=[f"bV{h % 2}"])
        P.dma("sp", Bt[h % 2][:].rearrange("p c s q -> p (c s q)"), b_d[h], w=[f"bB{h % 2}"])

    fin = make_fin(P, 1, ONt, lambda u: ("bON", u["blk"], u["h"]), lambda u: u["h"] * 64, None)
    units = []
    load_head(0)
    for h in range(16):
        pr, e_ = h // 2, h % 2
        Qh, Kh, Vh, Bh = Q[pr % 2], Kb[pr % 2], V[h % 2], Bt[h % 2]
        rk = [f"bQ{pr % 2}", f"bK{pr % 2}"]
        for blk in range(NTT):
            is_ctx = blk >= NLT
            q0 = blk * 128

            def sub(lc):
                return (Kh[e_ * 64:(e_ + 1) * 64, lc * 128:(lc + 1) * 128], Qh[e_ * 64:(e_ + 1) * 64, q0:q0 + 128],
                        Vh[:, lc, :], 0, e_)

            items = [dict(subs=[sub(0), sub(1)], kind="plain", rk=rk, vk=[f"bV{h % 2}"])]
            if not is_ctx:
                cl = na_class(blk)
                _, dstart, ns = NA_CLASSES[cl]
                for s0 in range(0, ns, 4):
                    s1 = min(ns, s0 + 4)
                    subs = [sub(4 + blk + dstart + s) for s in range(s0, s1)]
                    items.append(dict(subs=subs, kind="bias", rk=rk, vk=[f"bV{h % 2}"],
                                      aux=Bh[:, cl, s0:s1, :].rearrange("p s q -> p (s q)"),
                                      auxk=[f"bB{h % 2}"]))
            u = dict(items=items, G=1, fin=fin, blk=blk, h=h)
            if blk == 2 and h + 1 < 16:
                u["pre"] = (lambda hn=h + 1: load_head(hn))
            units.append(u)
    emit_local_units(P, units)
    for blk in range(NTT):
        P.dma("sp", on_d[blk * 128:(blk + 1) * 128, :], ONt[:, blk, :],
              r=[("bON", blk, h) for h in range(16)], is_output=True)


def run_att_B(pre_res, rpb_j):
    kT_all = np.concatenate([np.asarray(r["kT"])[:, :, :TPC] for r in pre_res], axis=2)
    kT_ctx = np.asarray(pre_res[0]["kT"])[:, :, TPC:]
    v_all = np.concatenate([np.asarray(r["v"])[:TPC] for r in pre_res], axis=0)
    v_ctx = np.asarray(pre_res[0]["v"])[TPC:]
    zk = np.zeros((8, 128, 256), dtype=kT_all.dtype)
    zv = np.zeros((256, v_all.shape[1]), dtype=v_all.dtype)
    in_maps = []
    for ci in range(NCORES):
        lo, hi = ci * TPC, (ci + 1) * TPC
        kb = kT_all[:, :, lo - 256:lo] if ci > 0 else zk
        ka = kT_all[:, :, hi:hi + 256] if ci < NCORES - 1 else zk
        vb = v_all[lo - 256:lo] if ci > 0 else zv
        va = v_all[hi:hi + 256] if ci < NCORES - 1 else zv
        kT = np.concatenate([kT_ctx, kb, kT_all[:, :, lo:hi], ka], axis=2)
        v = np.concatenate([v_ctx, vb, v_all[lo:hi], va], axis=0)
        v = v.reshape(NKC_B, 128, 16, 65).transpose(2, 1, 0, 3)
        bias = na_tables(rpb_j, ci).reshape(16, 128, 5 * 6 * 128)
        in_maps.append({"qT": np.asarray(pre_res[ci]["qT"]), "kT": np.ascontiguousarray(kT),
                        "v": np.ascontiguousarray(v), "bias": np.ascontiguousarray(bias)})
    return _run(build_att_B(), in_maps)


LAYER_MIX = ["A", "B", "C", "A"]
MOD_VECS = [(0, 0), (1, 0), (3, 0), (4, 0), (0, 1), (1, 1), (3, 1), (4, 1)]


def _phase_end(P):
    P.barrier()
    P.emit()


def emit_mod_phase(nc, P, cT, wblk, bblk, mod_loc, mod_all):
    outer = P.es
    with ExitStack() as es:
        P.es = es
        s_raw = _sb(P, "s_raw", [128, 16], F32)
        s = _sb(P, "s_act", [128, 16], F32)
        bt = _sb(P, "bt", [2, MOD_BLK * 512], F32)
        ot = _sb(P, "ot", [2, MOD_BLK * 512], F32)
        wt = [_sb(P, f"wt{i}", [128, 8, 512], F32) for i in range(2)]
        pst = [_ps(P, f"mps{i}", [128, 512], F32) for i in range(2)]
        P.dma("sp", s_raw[:], cT, w=["s_raw"])
        P.dma("sp", bt[:], bblk, w=["bt"])
        P.act(lambda e: e.activation(out=s[:], in_=s_raw[:], func=AF.Silu), r=["s_raw"], w=["s"])
        for b in range(MOD_BLK):
            w_ = wt[b % 2]
            pp = pst[b % 2]
            P.dma("sp", w_[:], wblk[b].rearrange("(k p) n -> p k n", p=128), w=[f"wt{b % 2}"])
            for k in range(8):
                P.pe(lambda e, k=k, w_=w_, pp=pp: e.matmul(pp[0:2, :], s[:, 2 * k:2 * k + 2], w_[:, k, :],
                                                          start=(k == 0), stop=(k == 7)),
                     r=["s", f"wt{b % 2}"], w=[f"PS:mps{b % 2}"])
            P.dve(lambda e, b=b, pp=pp: e.tensor_tensor(out=ot[:, b * 512:(b + 1) * 512], in0=pp[0:2, :],
                                                       in1=bt[:, b * 512:(b + 1) * 512], op=ALU.add),
                  r=[f"PS:mps{b % 2}", "bt"], w=[("ot", b)])
        P.dma("sp", mod_loc, ot[:], r=[("ot", b) for b in range(MOD_BLK)], w=["mod_loc"])
        P.cc(lambda e: e.collective_compute("AllGather", ALU.bypass, replica_groups=[list(range(NCORES))],
                                            ins=[mod_loc.opt()], outs=[mod_all.opt()]),
             r=["mod_loc"], w=["mod_all"])
        _phase_end(P)
    P.es = outer


def emit_consts(nc, P, i, mod_all, grow, modT_i, rows_i):
    def vec(ch, rr):
        row = (2 * i + ch // 3) * 2 + rr
        c0 = (ch % 3) * 1024
        return mod_all[row:row + 1, c0:c0 + 1024]
    for v, (ch, rr) in enumerate(MOD_VECS):
        P.dma("sp", modT_i[:, v, :], vec(ch, rr).rearrange("o (k p) -> p (o k)", p=128), w=[("modT_i", v)],
              allow_slow_non_contiguous=True)
    for ri, (ch, rr) in enumerate([(2, 0), (5, 0), (2, 1), (5, 1)]):
        P.dma("sp", rows_i[ri:ri + 1, :], vec(ch, rr), w=[("rows_i", ri)])
    for g in range(2):
        P.dma("sp", rows_i[4 + g:5 + g, :], grow[i, g:g + 1, :], w=[("rows_i", 4 + g)])
    _phase_end(P)


def emit_window(nc, P, mix, kloc, vloc, kpack, vpack, kpall, vpall, sel_d, kwin, vwin):
    cfg = MIX[mix]
    nkp, vh, dv = cfg["nkp"], cfg["vh"], cfg["dv"]
    HT = 2 if mix == "B" else 1
    HB = HT * 128
    outer = P.es
    with ExitStack() as es:
        P.es = es
        P.dma("sp", kpack[:, :, 0:HB], kloc[:, :, 0:HB], w=["kpack0"])
        P.dma("sp", kpack[:, :, HB:2 * HB], kloc[:, :, TPC - HB:TPC], w=["kpack1"])
        P.dma("sp", vpack[:, :, 0:HT, :], vloc[:, :, 0:HT, :], w=["vpack0"])
        P.dma("sp", vpack[:, :, HT:2 * HT, :], vloc[:, :, NLT - HT:NLT, :], w=["vpack1"])
        P.cc(lambda e: e.collective_compute("AllGather", ALU.bypass, replica_groups=[list(range(NCORES))],
                                            ins=[kpack.rearrange("a p t -> (a p) t").opt()],
                                            outs=[kpall.rearrange("r a p t -> (r a p) t").opt()]),
             r=["kpack0", "kpack1"], w=["kpall"])
        P.cc(lambda e: e.collective_compute("AllGather", ALU.bypass, replica_groups=[list(range(NCORES))],
                                            ins=[vpack.rearrange("h p c d -> (h p) (c d)").opt()],
                                            outs=[vpall.rearrange("r h p c d -> (r h p) (c d)").opt()]),
             r=["vpack0", "vpack1"], w=["vpall"])
        P.dma("sp", kwin[:, :, 0:CTX], kloc[:, :, TPC:TOKW], w=["kwin_c"])
        P.dma("sp", kwin[:, :, CTX + HB:CTX + HB + TPC], kloc[:, :, 0:TPC], w=["kwin_o"])
        P.dma("sp", vwin[:, :, 0:NCT, :], vloc[:, :, NLT:NTT, :], w=["vwin_c"])
        P.dma("sp", vwin[:, :, NCT + HT:NCT + HT + NLT, :], vloc[:, :, 0:NLT, :], w=["vwin_o"])
        sel = _sb(P, "sel", [128, 16], F32)
        P.dma("sp", sel[:], sel_d, w=["sel"])
        KP = _sb(P, "KP", [128, NCORES, nkp, 2 * HB], BF16)
        VP = _sb(P, "VP", [128, NCORES, vh, 2 * HT * (dv + 1)], BF16)
        for r in range(NCORES):
            P.dma("sp", KP[:, r, :, :], kpall[r].rearrange("a p t -> p a t"), r=["kpall"], w=[("KP", r)])
            P.dma("sp", VP[:, r, :, :], vpall[r].rearrange("h p c d -> p h (c d)"), r=["vpall"], w=[("VP", r)])
        hk = _sb(P, "hk", [128, 2, nkp, HB], BF16)
        hv = _sb(P, "hv", [128, 2, vh, HT * (dv + 1)], BF16)
        P.dve(lambda e: e.memset(hk[:], 0.0), w=["hk"])
        P.dve(lambda e: e.memset(hv[:], 0.0), w=["hv"])
        W1 = HT * (dv + 1)
        for side in range(2):
            for r in range(NCORES):
                so = HB if side == 0 else 0
                sv = W1 if side == 0 else 0
                P.dve(lambda e, side=side, r=r, so=so: e.scalar_tensor_tensor(
                    out=hk[:, side, :, :], in0=KP[:, r, :, so:so + HB], scalar=sel[:, side * 8 + r:side * 8 + r + 1],
                    in1=hk[:, side, :, :], op0=ALU.mult, op1=ALU.add), r=[("KP", r), "sel", "hk"], w=["hk"])
                P.dve(lambda e, side=side, r=r, sv=sv: e.scalar_tensor_tensor(
                    out=hv[:, side, :, :], in0=VP[:, r, :, sv:sv + W1], scalar=sel[:, side * 8 + r:side * 8 + r + 1],
                    in1=hv[:, side, :, :], op0=ALU.mult, op1=ALU.add), r=[("VP", r), "sel", "hv"], w=["hv"])
        for side in range(2):
            t0 = CTX if side == 0 else CTX + HB + TPC
            c0 = NCT if side == 0 else NCT + HT + NLT
            P.dma("sp", kwin[:, :, t0:t0 + HB].rearrange("a p t -> p a t"), hk[:, side, :, :], r=["hk"],
                  w=[("kwin_h", side)])
            P.dma("sp", vwin[:, :, c0:c0 + HT, :].rearrange("h p c d -> p h (c d)"), hv[:, side, :, :], r=["hv"],
                  w=[("vwin_h", side)])
        _phase_end(P)
    P.es = outer


def build_fused(nc, P):
    dt_in = lambda name, shape, dt=F32: nc.dram_tensor(name, shape, dt, kind="ExternalInput").ap()
    dt_i = lambda name, shape, dt=F32: nc.dram_tensor(name, shape, dt, kind="Internal").ap()
    xin = dt_in("xin", [TOKW, D])
    cT = dt_in("cT", [128, 16])
    wblk = dt_in("wblk", [MOD_BLK, D, 512])
    bblk = dt_in("bblk", [2, MOD_BLK * 512])
    gT_all = dt_in("gT_all", [DEPTH, 128, 4, 8])
    grow = dt_in("grow", [DEPTH, 2, D])
    id_d = dt_in("ident", [128, 128])
    rc_d = dt_in("ropeC", [TPC, 64])
    rs_d = dt_in("ropeS", [TPC, 64])
    wqkv = {0: dt_in("wqkv0", [D, 3072]), 1: dt_in("wqkv1", [D, 3072]), 2: dt_in("wqkv2", [D, 1536]),
            3: dt_in("wqkv3", [D, 3072])}
    wo = [dt_in(f"wo{i}", [D, D]) for i in range(DEPTH)]
    w13 = [dt_in(f"w13_{i}", [D, 2 * FFN]) for i in range(DEPTH)]
    w2 = [dt_in(f"w2_{i}", [FFN, D]) for i in range(DEPTH)]
    lam_d = [dt_in("lam0", [128, 256]), dt_in("lam1", [128, 256])]
    sub_d = [dt_in("sub0", [128, 128]), dt_in("sub1", [128, 128])]
    bias_d = dt_in("nabias", [16, 128, 5 * 6 * 128])
    mk_d = dt_in("cmasks", [128, 4, 128])
    sk_d = dt_in("csink", [128, 16])
    selB = dt_in("sel", [128, 16])
    y_d = nc.dram_tensor("y", [TPC, D], F32, kind="ExternalOutput").ap()
    xs = dt_i("xs", [TOKW, D])
    xm = dt_i("xmid", [TOKW, D])
    on = dt_i("on", [TOKW, D], BF16)
    mod_loc = dt_i("mod_loc", [2, MOD_BLK * 512])
    mod_all = dt_i("mod_all", [2 * NCORES, MOD_BLK * 512])
    modT_i = [dt_i(f"modT_{i}", [128, 8, 8]) for i in range(DEPTH)]
    rows_i = [dt_i(f"rows_{i}", [6, D]) for i in range(DEPTH)]
    qT = dt_i("qT", [8, 128, TOKW], BF16)

    for t in range(0, NTT, 6):
        P.dma("sp", xs[t * 128:(t + 6) * 128, :], xin[t * 128:(t + 6) * 128, :], w=[("xs0", t)])
    emit_mod_phase(nc, P, cT, wblk, bblk, mod_loc, mod_all)
    for i in range(DEPTH):
        mix = LAYER_MIX[i]
        cfg = MIX[mix]
        nkp, vh, dv = cfg["nkp"], cfg["vh"], cfg["dv"]
        emit_consts(nc, P, i, mod_all, grow, modT_i[i], rows_i[i])
        kloc = dt_i(f"kloc{i}", [nkp, 128, TOKW], BF16)
        vloc = dt_i(f"vloc{i}", [vh, 128, NTT, dv + 1], BF16)
        with ExitStack() as es:
            P.es = es
            emit_pre(nc, P, mix, xs, wqkv[i], modT_i[i], gT_all[i], id_d, rc_d, rs_d, qT, kloc, vloc)
            _phase_end(P)
        P.es = P.es_root
        if mix == "A":
            kall = dt_i(f"kall{i}", [NCORES, nkp * 128, TOKW], BF16)
            vall = dt_i(f"vall{i}", [NCORES, vh * 128, NTT * (dv + 1)], BF16)
            P.cc(lambda e, kloc=kloc, kall=kall: e.collective_compute(
                "AllGather", ALU.bypass, replica_groups=[list(range(NCORES))],
                ins=[kloc.rearrange("a p t -> (a p) t").opt()], outs=[kall.rearrange("r q t -> (r q) t").opt()]),
                w=["kall"])
            P.cc(lambda e, vloc=vloc, vall=vall: e.collective_compute(
                "AllGather", ALU.bypass, replica_groups=[list(range(NCORES))],
                ins=[vloc.rearrange("h p c d -> (h p) (c d)").opt()], outs=[vall.rearrange("r q t -> (r q) t").opt()]),
                w=["vall"])
            _phase_end(P)
            lam_init = 0.8 - 0.6 * math.exp(-0.3 * i)
            with ExitStack() as es:
                P.es = es
                emit_att_A(nc, P, lam_init, qT, kloc, vloc, kall, vall, lam_d[i // 3], sub_d[i // 3], on)
                _phase_end(P)
            P.es = P.es_root
        else:
            HT = 2 if mix == "B" else 1
            nwin = (NCT + 2 * HT + NLT)
            kpack = dt_i(f"kpack{i}", [nkp, 128, 2 * HT * 128], BF16)
            vpack = dt_i(f"vpack{i}", [vh, 128, 2 * HT, dv + 1], BF16)
            kpall = dt_i(f"kpall{i}", [NCORES, nkp, 128, 2 * HT * 128], BF16)
            vpall = dt_i(f"vpall{i}", [NCORES, vh, 128, 2 * HT, dv + 1], BF16)
            kwin = dt_i(f"kwin{i}", [nkp, 128, nwin * 128], BF16)
            vwin = dt_i(f"vwin{i}", [vh, 128, nwin, dv + 1], BF16)
            emit_window(nc, P, mix, kloc, vloc, kpack, vpack, kpall, vpall, selB, kwin, vwin)
            with ExitStack() as es:
                P.es = es
                if mix == "B":
                    emit_att_B(nc, P, qT, kwin, vwin, bias_d, on)
                else:
                    emit_att_C(nc, P, qT, kwin, vwin, mk_d, sk_d, on)
                _phase_end(P)
            P.es = P.es_root
        emit_post(nc, P, xs, on, wo[i], w13[i], w2[i], modT_i[i], gT_all[i], rows_i[i], id_d, xm, xs)
    P.es = P.es_root
    for t in range(0, NLT, 4):
        P.dma("sp", y_d[t * 128:(t + 4) * 128, :], xs[t * 128:(t + 4) * 128, :], is_output=True)


def kernel(x, c, ctx, c_ctx, ada_w, ada_b, norm_g, ffn_w13, ffn_w2, a_wqkv, a_wo, a_lambda, a_subln,
           b_wqkv, b_wo, b_rpb, c_wqkv, c_wo, c_sink):
    f32 = lambda a: np.ascontiguousarray(np.asarray(a, dtype=np.float32))
    x = f32(x)[0]
    ctx = f32(ctx)[0]
    norm_g = f32(norm_g)
    ada_w, ada_b = f32(ada_w), f32(ada_b)
    cc_ = np.stack([f32(c).reshape(D), f32(c_ctx).reshape(D)], axis=0)
    cT = np.ascontiguousarray(cc_.reshape(2, 8, 128).transpose(2, 1, 0)).reshape(128, 16)
    wb = np.ascontiguousarray(ada_w.reshape(DEPTH, D, 12, 512).transpose(0, 2, 1, 3)).reshape(48, D, 512)
    bb = ada_b.reshape(48, 512)
    gT_all = np.ascontiguousarray(np.stack(
        [np.stack([_fm(norm_g[i, j]) for j in range(4)], axis=1) for i in range(DEPTH)], axis=0))
    grow = np.ascontiguousarray(norm_g[:, [1, 3], :])
    C, S = rope_tables()
    ident = np.eye(128, dtype=np.float32)
    bc = lambda v, n: np.ascontiguousarray(np.broadcast_to(np.asarray(v, np.float32).reshape(1, n), (128, n)))
    p = np.arange(128)
    m_lo = (p[:, None] >= p[None, :]).astype(np.float32)
    m_hi = (p[:, None] <= p[None, :]).astype(np.float32)
    shared = {
        "cT": cT, "gT_all": gT_all, "grow": grow, "ident": ident,
        "wqkv0": f32(a_wqkv[0]), "wqkv1": f32(b_wqkv[0]), "wqkv2": f32(c_wqkv[0]), "wqkv3": f32(a_wqkv[1]),
        "wo0": f32(a_wo[0]), "wo1": f32(b_wo[0]), "wo2": f32(c_wo[0]), "wo3": f32(a_wo[1]),
        "lam0": bc(a_lambda[0], 256), "lam1": bc(a_lambda[1], 256),
        "sub0": bc(a_subln[0], 128), "sub1": bc(a_subln[1], 128), "csink": bc(c_sink[0], 16),
    }
    for i in range(DEPTH):
        shared[f"w13_{i}"] = f32(ffn_w13[i])
        shared[f"w2_{i}"] = f32(ffn_w2[i])
    rpb = f32(b_rpb[0])
    in_maps = []
    for ci in range(NCORES):
        sl = slice(ci * MOD_BLK, (ci + 1) * MOD_BLK)
        brow = bb[sl].reshape(1, MOD_BLK * 512)
        sel = np.zeros((128, 16), np.float32)
        if ci > 0:
            sel[:, ci - 1] = 1.0
        if ci < NCORES - 1:
            sel[:, 8 + ci + 1] = 1.0
        masks = np.stack([m_lo, m_hi, m_lo if ci > 0 else np.zeros_like(m_lo),
                          m_hi if ci < NCORES - 1 else np.zeros_like(m_hi)], axis=1)
        m = dict(shared)
        m.update({
            "xin": np.ascontiguousarray(np.concatenate([x[ci * TPC:(ci + 1) * TPC], ctx], axis=0)),
            "wblk": np.ascontiguousarray(wb[sl]),
            "bblk": np.ascontiguousarray(np.concatenate([brow, brow], axis=0)),
            "ropeC": np.ascontiguousarray(C[ci * TPC:(ci + 1) * TPC]),
            "ropeS": np.ascontiguousarray(S[ci * TPC:(ci + 1) * TPC]),
            "nabias": np.ascontiguousarray(na_tables(rpb, ci).reshape(16, 128, 5 * 6 * 128)),
            "cmasks": np.ascontiguousarray(masks).astype(np.float32),
            "sel": sel,
        })
        in_maps.append(m)
    nc = bass.Bass("TRN2", target_bir_lowering=False, num_devices=NCORES)
    with ExitStack() as es:
        P = Prog(nc, es)
        build_fused(nc, P)
        P.finish()
    res = run_bass_kernel_spmd(nc, in_maps, core_ids=list(range(NCORES))).results
    y = np.concatenate([np.asarray(r["y"]) for r in res], axis=0)
    return np.ascontiguousarray(y[None]).astype(np.float32)
```

```python
import math
from contextlib import ExitStack

import numpy as np
import ml_dtypes

import concourse.bass as bass
import concourse.mybir as mybir
from concourse.bass_utils import run_bass_kernel_spmd

F32 = mybir.dt.float32
BF16 = mybir.dt.bfloat16
AF = mybir.ActivationFunctionType
ALU = mybir.AluOpType
AX = mybir.AxisListType

NCORES = 8
D = 1024
SEQ = 16384
CTX = 256
TPC = SEQ // NCORES
NLT = TPC // 128
NCT = CTX // 128
NTT = NLT + NCT
DEPTH = 4
FFN = 2816
EPS = 1e-6
GRID_W = 64
NDMA_SEM = 12


class Prog:
    def __init__(self, nc, es):
        self.nc = nc
        self.es = es
        self.es_root = es
        self.ops = []
        self.lastw = {}
        self.readers = {}
        self.eng = {"pe": nc.tensor, "act": nc.scalar, "dve": nc.vector, "pool": nc.gpsimd, "sp": nc.sync}
        self.out_dmas = []

    def add(self, eng, fn, r=(), w=(), dma=False):
        idx = len(self.ops)
        deps = set()
        psk = [k for k in list(r) + list(w) if isinstance(k, str) and k.startswith("PS:")]
        r = [k for k in r if k not in psk]
        w = [k for k in w if k not in psk]

        def need(di, kind):
            d = self.ops[di]
            if d["dma"] or dma:
                return True
            if d["eng"] == eng:
                return eng != "pe"
            return True

        for k in psk:
            lw = self.lastw.get(k)
            if lw is not None and self.ops[lw]["eng"] != eng:
                deps.add(lw)
            self.lastw[k] = idx

        for k in r:
            lw = self.lastw.get(k)
            if lw is not None and need(lw, "raw"):
                deps.add(lw)
        for k in w:
            lw = self.lastw.get(k)
            if lw is not None and need(lw, "waw"):
                deps.add(lw)
            for rd in self.readers.get(k, ()):
                if rd != idx and need(rd, "war"):
                    deps.add(rd)
        for k in r:
            lst = self.readers.setdefault(k, [])
            if not dma:
                lst[:] = [j for j in lst if self.ops[j]["dma"] or self.ops[j]["eng"] != eng]
            lst.append(idx)
        for k in w:
            self.lastw[k] = idx
            self.readers[k] = []
        for d in deps:
            self.ops[d]["mark"] = True
        self.ops.append(dict(eng=eng, fn=fn, dma=dma, deps=deps, mark=False))
        return idx

    def pe(self, fn, r=(), w=()):
        return self.add("pe", fn, r, w)

    def act(self, fn, r=(), w=(), embed=False):
        i = self.add("act", fn, r, w)
        if embed:
            self.ops[i]["embed"] = True
        return i

    def dve(self, fn, r=(), w=()):
        return self.add("dve", fn, r, w)

    def pool(self, fn, r=(), w=()):
        return self.add("pool", fn, r, w)

    def dma(self, q, out, in_, r=(), w=(), is_output=False, **kw):
        i = self.add(q, lambda e: e.dma_start(out=out, in_=in_, **kw), r, w, dma=True)
        if is_output:
            self.out_dmas.append(i)
        return i

    def cc(self, fn, r=(), w=()):
        i = self.add("pool", fn, r, w, dma=True)
        self.ops[i]["cc"] = True
        return i

    def barrier(self):
        n = len(self.ops)
        last = {}
        for i, op in enumerate(self.ops):
            if op["fn"] is None:
                continue
            if op["dma"]:
                last.setdefault(("dma", i), i)
            else:
                last[op["eng"]] = i
        prev = getattr(self, "_last_barrier", 0)
        deps = set()
        for k, i in last.items():
            if isinstance(k, tuple):
                if i >= prev:
                    deps.add(i)
            else:
                deps.add(i)
        for d in deps:
            self.ops[d]["mark"] = True
        for e in ("pe", "act", "dve", "pool", "sp"):
            self.ops.append(dict(eng=e, fn=None, dma=False, deps=set(deps), mark=False))
        self._last_barrier = n
        self.lastw = {}
        self.readers = {}

    def finish(self):
        deps = set(self.out_dmas)
        last = {}
        for i, op in enumerate(self.ops):
            if op["fn"] is not None:
                last[op["eng"]] = i
        deps.update(last.values())
        for d in deps:
            self.ops[d]["mark"] = True
        self.ops.append(dict(eng="sp", fn=None, dma=False, deps=deps, mark=False))
        self.emit()

    def emit(self):
        nc, es = self.nc, self.es_root
        if not hasattr(self, "_st"):
            sems = {e: es.enter_context(nc.semaphore(f"sem_{e}")) for e in ("pe", "act", "dve", "pool", "cc")}
            dsem = {q: [es.enter_context(nc.semaphore(f"dsem_{q}{i}")) for i in range(NDMA_SEM)]
                    for q in ("sp", "pool", "act")}
            semobj = {}
            for e, s in sems.items():
                semobj[f"c_{e}"] = s
            for q, lst in dsem.items():
                for i, s in enumerate(lst):
                    semobj[f"d_{q}{i}"] = s
            self._st = dict(sems=sems, semobj=semobj, cnt={e: 0 for e in sems},
                            dman={q: 0 for q in dsem}, waited={}, done={}, pos=0)
        st = self._st
        sems, semobj, cnt, dman, waited, done = (st["sems"], st["semobj"], st["cnt"], st["dman"],
                                                 st["waited"], st["done"])
        start = st["pos"]
        st["pos"] = len(self.ops)
        for i in range(start, len(self.ops)):
            op = self.ops[i]
            ename = op["eng"]
            eng = self.eng[ename]
            waits = {}
            for d in op["deps"]:
                sname, val = done[d]
                if waits.get(sname, 0) < val:
                    waits[sname] = val
            if op.get("cc"):
                pass
            elif op["dma"]:
                m = dman[ename]
                dman[ename] += 1
                sname = f"d_{ename}{m % NDMA_SEM}"
                val = 16 * (m // NDMA_SEM + 1)
                if m >= NDMA_SEM and waits.get(sname, 0) < val - 16:
                    waits[sname] = val - 16
            pend = [(sn, v) for sn, v in waits.items() if waited.get((ename, sn), 0) < v]
            emb = None
            if op.get("embed") and op["fn"] is not None and pend:
                emb = pend.pop()
            for sn, v in pend:
                eng.wait_ge(semobj[sn], v)
                waited[(ename, sn)] = v
            if op["fn"] is None:
                continue
            ins = op["fn"](eng)
            if emb is not None:
                ins._wait_ge(semobj[emb[0]], emb[1])
                waited[(ename, emb[0])] = emb[1]
            if op.get("cc"):
                cnt["cc"] += 1
                ins.then_inc(sems["cc"], 1)
                done[i] = ("c_cc", cnt["cc"])
            elif op["dma"]:
                ins.then_inc(semobj[sname], 16)
                done[i] = (sname, val)
            elif op["mark"]:
                cnt[ename] += 1
                ins.then_inc(sems[ename], 1)
                done[i] = (f"c_{ename}", cnt[ename])
        self.nmarks = dict(cnt)


def _uid(P):
    P._uidc = getattr(P, "_uidc", 0) + 1
    return P._uidc


def _sb(P, name, shape, dt):
    return P.es.enter_context(P.nc.sbuf_tensor(f"s{_uid(P)}_{name}", shape, dt))


def _ps(P, name, shape, dt):
    return P.es.enter_context(P.nc.psum_tensor(f"p{_uid(P)}_{name}", shape, dt))


def _run(build_fn, in_maps):
    nc = bass.Bass("TRN2", target_bir_lowering=False)
    with ExitStack() as es:
        P = Prog(nc, es)
        build_fn(nc, P)
        P.finish()
    res = run_bass_kernel_spmd(nc, in_maps, core_ids=list(range(NCORES)))
    return res.results


MOD_BLK = 6


def build_mod(nc, P):
    cT = nc.dram_tensor("cT", [128, 16], F32, kind="ExternalInput").ap()
    wblk = nc.dram_tensor("wblk", [MOD_BLK, D, 512], F32, kind="ExternalInput").ap()
    bblk = nc.dram_tensor("bblk", [2, MOD_BLK * 512], F32, kind="ExternalInput").ap()
    out = nc.dram_tensor("modo", [2, MOD_BLK * 512], F32, kind="ExternalOutput").ap()
    s_raw = _sb(P, "s_raw", [128, 16], F32)
    s = _sb(P, "s_act", [128, 16], F32)
    bt = _sb(P, "bt", [2, MOD_BLK * 512], F32)
    ot = _sb(P, "ot", [2, MOD_BLK * 512], F32)
    wt = [_sb(P, f"wt{i}", [128, 8, 512], F32) for i in range(2)]
    pst = [_ps(P, f"ps{i}", [128, 512], F32) for i in range(2)]
    P.dma("sp", s_raw[:], cT, w=["s_raw"])
    P.dma("sp", bt[:], bblk, w=["bt"])
    P.act(lambda e: e.activation(out=s[:], in_=s_raw[:], func=AF.Silu), r=["s_raw"], w=["s"])
    for b in range(MOD_BLK):
        w_ = wt[b % 2]
        pp = pst[b % 2]
        P.dma("sp", w_[:], wblk[b].rearrange("(k p) n -> p k n", p=128), w=[f"wt{b % 2}"])
        for k in range(8):
            P.pe(lambda e, k=k, w_=w_, pp=pp: e.matmul(pp[0:2, :], s[:, 2 * k:2 * k + 2], w_[:, k, :],
                                                      start=(k == 0), stop=(k == 7)),
                 r=["s", f"wt{b % 2}"], w=[f"ps{b % 2}"])
        P.dve(lambda e, b=b, pp=pp: e.tensor_tensor(out=ot[:, b * 512:(b + 1) * 512], in0=pp[0:2, :],
                                                   in1=bt[:, b * 512:(b + 1) * 512], op=ALU.add),
              r=[f"ps{b % 2}", "bt"], w=[("ot", b)])
    P.dma("sp", out, ot[:], r=[("ot", b) for b in range(MOD_BLK)], is_output=True)


def run_mod(c, c_ctx, ada_w, ada_b):
    cc = np.stack([c.reshape(D), c_ctx.reshape(D)], axis=0)
    cT = np.ascontiguousarray(cc.reshape(2, 8, 128).transpose(2, 1, 0)).reshape(128, 16)
    wb = np.ascontiguousarray(ada_w.reshape(DEPTH, D, 12, 512).transpose(0, 2, 1, 3)).reshape(48, D, 512)
    bb = ada_b.reshape(48, 512)
    in_maps = []
    for ci in range(NCORES):
        sl = slice(ci * MOD_BLK, (ci + 1) * MOD_BLK)
        brow = bb[sl].reshape(1, MOD_BLK * 512)
        in_maps.append({"cT": cT, "wblk": np.ascontiguousarray(wb[sl]),
                        "bblk": np.ascontiguousarray(np.concatenate([brow, brow], axis=0))})
    res = _run(build_mod, in_maps)
    mod = np.concatenate([r["modo"].reshape(2, MOD_BLK, 512) for r in res], axis=1)
    return mod.reshape(2, DEPTH, 6 * D)


MIX = {
    "A": dict(ncols=3072, rope=True, nqp=8, nkp=8, vh=8, dv=128, koff=1024, kcols=1024, voff=2048),
    "B": dict(ncols=3072, rope=False, nqp=8, nkp=8, vh=16, dv=64, koff=1024, kcols=1024, voff=2048),
    "C": dict(ncols=1536, rope=True, nqp=8, nkp=4, vh=4, dv=64, koff=1024, kcols=256, voff=1280),
}
TOKW = NTT * 128


def _bc(ap, shape):
    return ap.broadcast_to(shape)


def emit_norm_T(P, xt, xkey, hT, hkey, Asc, Bsc, ident, bufs, tag):
    junk, ssq, lnv, rstd, xn, ptr = bufs
    import os
    SUB = int(os.environ.get("PRE_SUB", "9"))
    P.act(lambda e: e.activation(out=junk[:], in_=xt, func=AF.Square, accum_out=ssq[:, 0:1]),
          r=[xkey], w=[tag + "junk", tag + "ssq"])
    if SUB < 2:
        return
    P.act(lambda e: e.activation(out=lnv[:, 0:1], in_=ssq[:, 0:1], func=AF.Ln, scale=1.0 / D, bias=EPS),
          r=[tag + "ssq"], w=[tag + "lnv"])
    if SUB < 3:
        return
    P.act(lambda e: e.activation(out=rstd[:, 0:1], in_=lnv[:, 0:1], func=AF.Exp, scale=-0.5),
          r=[tag + "lnv"], w=[tag + "rstd"])
    P.dve(lambda e: e.tensor_scalar(out=xn[:], in0=xt, scalar1=rstd[:, 0:1], scalar2=None, op0=ALU.mult),
          r=[xkey, tag + "rstd"], w=[tag + "xn"])
    if SUB < 4:
        return
    for fc in range(8):
        P.pe(lambda e, fc=fc: e.transpose(ptr[:, fc * 128:(fc + 1) * 128], xn[:, fc * 128:(fc + 1) * 128], ident[:]),
             r=[tag + "xn", "ident"], w=["PS:" + tag + "ptr"])
    if SUB < 5:
        return
    EV = os.environ.get("EVAC", "dve")
    for fc in range(8):
        if (fc % 2 == 0 and EV == "mixed") or EV == "dve":
            P.dve(lambda e, fc=fc: e.tensor_scalar(out=hT[:, fc, :], in0=ptr[:, fc * 128:(fc + 1) * 128],
                                                  scalar1=Asc[:, fc:fc + 1], scalar2=Bsc[:, fc:fc + 1],
                                                  op0=ALU.mult, op1=ALU.add),
                  r=["PS:" + tag + "ptr", "modsc"], w=[(hkey, fc)])
        else:
            P.act(lambda e, fc=fc: e.activation(out=hT[:, fc, :], in_=ptr[:, fc * 128:(fc + 1) * 128],
                                               func=AF.Identity, scale=Asc[:, fc:fc + 1], bias=Bsc[:, fc:fc + 1]),
                  r=["PS:" + tag + "ptr", "modsc"], w=[(hkey, fc)])


def emit_modsc(P, modT_d, gT_d, which):
    mt = _sb(P, "modT", [128, 8, 8], F32)
    gt = _sb(P, "gT", [128, 4, 8], F32)
    A = _sb(P, "modA", [128, 2, 8], F32)
    P.dma("sp", mt[:], modT_d, w=["modT_raw"])
    P.dma("sp", gt[:], gT_d, w=["gT_raw"])
    sh, sc = 2 * which, 2 * which + 1
    for ci in range(2):
        P.dve(lambda e, ci=ci: e.tensor_scalar(out=A[:, ci, :], in0=mt[:, 4 * ci + sc, :], scalar1=1.0,
                                              scalar2=None, op0=ALU.add),
              r=["modT_raw"], w=[("modA0", ci)])
        P.dve(lambda e, ci=ci: e.tensor_tensor(out=A[:, ci, :], in0=A[:, ci, :], in1=gt[:, 2 * which, :],
                                              op=ALU.mult),
              r=[("modA0", ci), "gT_raw"], w=["modsc"] if ci == 1 else [("modA1", ci)])
    return A[:, 0, :], mt[:, sh, :], A[:, 1, :], mt[:, 4 + sh, :]


def build_pre(mix):
    cfg = MIX[mix]
    ncols, nqp, nkp, vh, dv = cfg["ncols"], cfg["nqp"], cfg["nkp"], cfg["vh"], cfg["dv"]
    vw = vh * (dv + 1)

    def build(nc, P):
        x_d = nc.dram_tensor("x", [TOKW, D], F32, kind="ExternalInput").ap()
        w_d = nc.dram_tensor("wqkv", [D, ncols], F32, kind="ExternalInput").ap()
        modT_d = nc.dram_tensor("modT", [128, 8, 8], F32, kind="ExternalInput").ap()
        gT_d = nc.dram_tensor("gT", [128, 4, 8], F32, kind="ExternalInput").ap()
        id_d = nc.dram_tensor("ident", [128, 128], F32, kind="ExternalInput").ap()
        rc_d = nc.dram_tensor("ropeC", [TPC, 64], F32, kind="ExternalInput").ap()
        rs_d = nc.dram_tensor("ropeS", [TPC, 64], F32, kind="ExternalInput").ap()
        qT_d = nc.dram_tensor("qT", [nqp, 128, TOKW], BF16, kind="ExternalOutput").ap()
        kT_d = nc.dram_tensor("kT", [nkp, 128, TOKW], BF16, kind="ExternalOutput").ap()
        v_d = nc.dram_tensor("v", [vh, 128, NTT, dv + 1], BF16, kind="ExternalOutput").ap()
        emit_pre(nc, P, mix, x_d, w_d, modT_d, gT_d, id_d, rc_d, rs_d, qT_d, kT_d, v_d)
    return build


def emit_pre(nc, P, mix, x_d, w_d, modT_d, gT_d, id_d, rc_d, rs_d, qT_d, kT_d, v_d):
    cfg = MIX[mix]
    ncols, nqp, nkp, vh, dv = cfg["ncols"], cfg["nqp"], cfg["nkp"], cfg["vh"], cfg["dv"]
    koff, kcols, voff = cfg["koff"], cfg["kcols"], cfg["voff"]
    nch = ncols // 512
    ident = _sb(P, "ident", [128, 128], BF16)
    P.dma("pool", ident[:], id_d, w=["ident"])
    W = _sb(P, "W", [128, 8, ncols], BF16)
    for k in range(8):
        P.dma("pool", W[:, k, :], w_d[k * 128:(k + 1) * 128, :], w=[("W", k)])
    Wkeys = [("W", k) for k in range(8)]
    A_l, B_l, A_c, B_c = emit_modsc(P, modT_d, gT_d, 0)
    ropeC = _sb(P, "ropeC", [128, NLT, 64], F32)
    ropeS = _sb(P, "ropeS", [128, NLT, 64], F32)
    P.dma("sp", ropeC[:], rc_d.rearrange("(t p) j -> p t j", p=128), w=["ropeC"])
    P.dma("sp", ropeS[:], rs_d.rearrange("(t p) j -> p t j", p=128), w=["ropeS"])
    xb = [_sb(P, f"xb{i}", [128, D], F32) for i in range(2)]
    junk = _sb(P, "junk", [128, D], BF16)
    ssq = _sb(P, "ssq", [128, 1], F32)
    lnv = _sb(P, "lnv", [128, 1], F32)
    rstd = _sb(P, "rstd", [128, 1], F32)
    xn = _sb(P, "xn", [128, D], BF16)
    ptr = _ps(P, "ptr", [128, D], BF16)
    hT = [_sb(P, f"hT{i}", [128, 8, 128], BF16) for i in range(2)]
    pq = [_ps(P, f"pq{i}", [128, 512], F32) for i in range(3)]
    t1 = _sb(P, "t1", [128, 512], F32)
    t2 = _sb(P, "t2", [128, 512], F32)
    qk = _sb(P, "qk", [128, 2048], BF16)
    ptr2 = [_ps(P, f"ptr2{i}", [128, 1024], BF16) for i in range(2)]
    GT = 4
    qTg = _sb(P, "qTg", [128, nqp, GT * 128], BF16)
    kTg = _sb(P, "kTg", [128, nkp, GT * 128], BF16)
    vt = [_sb(P, f"vt{i}", [128, vh, dv + 1], BF16) for i in range(2)]
    for i in range(2):
        P.dve(lambda e, i=i: e.memset(vt[i][:], 1.0), w=[f"vt{i}"])
    nkt = nkp
    import os
    STG = int(os.environ.get("PRE_STAGE", "9"))
    for t in range(NTT if STG >= 9 else 1):
        is_ctx = t >= NLT
        xt = xb[t % 2]
        xkey = f"xb{t % 2}"
        P.dma("sp", xt[:], x_d[t * 128:(t + 1) * 128, :], w=[xkey])
        h = hT[t % 2]
        hkey = f"hT{t % 2}"
        if STG < 2:
            break
        emit_norm_T(P, xt[:], xkey, h, hkey, A_c if is_ctx else A_l, B_c if is_ctx else B_l, ident,
                    (junk, ssq, lnv, rstd, xn, ptr), "n")
        hkeys = [(hkey, fc) for fc in range(8)]
        g = t % GT
        if STG < 3:
            break
        for c in range(nch):
            pp = pq[c % 3]
            pkey = f"PS:pq{c % 3}"
            for k in range(8):
                P.pe(lambda e, k=k, pp=pp, c=c, h=h: e.matmul(pp[:], h[:, k, :], W[:, k, c * 512:(c + 1) * 512],
                                                             start=(k == 0), stop=(k == 7)),
                     r=hkeys + Wkeys, w=[pkey])
            c0 = c * 512
            if c0 >= voff:
                nh = 512 // dv
                h0 = (c0 - voff) // dv
                vv = vt[t % 2]
                P.act(lambda e, pp=pp, vv=vv, h0=h0, nh=nh: e.activation(
                    out=vv[:, h0:h0 + nh, 0:dv], in_=pp[:].rearrange("p (h d) -> p h d", d=dv), func=AF.Copy),
                    r=[pkey], w=[f"vt{t % 2}"])
                continue
            segs = []
            if mix == "C" and c == 2:
                segs.append(("k", 0, 256))
                segs.append(("v", 256, 256))
            else:
                segs.append(("q" if c0 < koff else "k", 0, 512))
            for kind, s0, sl in segs:
                if kind == "v":
                    vv = vt[t % 2]
                    P.act(lambda e, pp=pp, vv=vv, s0=s0, sl=sl: e.activation(
                        out=vv[:, :, 0:dv], in_=pp[:, s0:s0 + sl].rearrange("p (h d) -> p h d", d=dv),
                        func=AF.Copy), r=[pkey], w=[f"vt{t % 2}"])
                    continue
                nh = sl // 64
                src = pp[:, s0:s0 + sl]
                if kind == "q":
                    dsts = [qk[:, c0:c0 + sl]]
                elif mix == "C":
                    base = qk[:, 1024:1024 + 512].rearrange("p (h two d) -> p h two d", two=2, d=64)
                    dsts = [base[:, :, 0, :], base[:, :, 1, :]]
                else:
                    dsts = [qk[:, c0:c0 + sl]]
                if cfg["rope"] and not is_ctx:
                    x3 = src.rearrange("p (h d) -> p h d", d=64)
                    x5 = src.rearrange("p (h a b j) -> p h a b j", a=2, b=2, j=16)
                    Cb = _bc(ropeC[:, t, :].unsqueeze(1), [128, nh, 64])
                    S5 = ropeS[:, t, :].rearrange("p (a b j) -> p a b j", a=2, b=2, j=16)
                    t13 = t1[:, 0:sl].rearrange("p (h d) -> p h d", d=64)
                    t25 = t2[:, 0:sl].rearrange("p (h a b j) -> p h a b j", a=2, b=2, j=16)
                    P.dve(lambda e, x3=x3, Cb=Cb, t13=t13: e.tensor_tensor(out=t13, in0=x3, in1=Cb, op=ALU.mult),
                          r=[pkey, "ropeC"], w=["t1"])
                    for b in range(2):
                        Sb = _bc(S5[:, :, b, :].unsqueeze(1), [128, nh, 2, 16])
                        P.dve(lambda e, x5=x5, Sb=Sb, t25=t25, b=b: e.tensor_tensor(
                            out=t25[:, :, :, b, :], in0=x5[:, :, :, 1 - b, :], in1=Sb, op=ALU.mult),
                            r=[pkey, "ropeS"], w=[("t2", b)])
                    for dst in dsts:
                        if len(dsts) == 1:
                            P.dve(lambda e, dst=dst, sl=sl: e.tensor_tensor(out=dst, in0=t1[:, 0:sl], in1=t2[:, 0:sl],
                                                                         op=ALU.add),
                                  r=["t1", ("t2", 0), ("t2", 1)], w=["qk"])
                        else:
                            P.dve(lambda e, dst=dst, sl=sl: e.tensor_tensor(
                                out=dst, in0=t1[:, 0:sl].rearrange("p (h d) -> p h d", d=64),
                                in1=t2[:, 0:sl].rearrange("p (h d) -> p h d", d=64), op=ALU.add),
                                r=["t1", ("t2", 0), ("t2", 1)], w=["qk"])
                else:
                    for dst in dsts:
                        if len(dsts) == 1:
                            P.act(lambda e, dst=dst, src=src: e.activation(out=dst, in_=src, func=AF.Copy),
                                  r=[pkey], w=["qk"])
                        else:
                            P.act(lambda e, dst=dst, src=src: e.activation(
                                out=dst, in_=src.rearrange("p (h d) -> p h d", d=64), func=AF.Copy),
                                r=[pkey], w=["qk"])
        if STG < 4:
            break
        for j in range(8):
            P.pe(lambda e, j=j: e.transpose(ptr2[0][:, j * 128:(j + 1) * 128], qk[:, j * 128:(j + 1) * 128], ident[:]),
                 r=["qk", "ident"], w=["PS:ptr2_0"])
        for j in range(nkt):
            P.pe(lambda e, j=j: e.transpose(ptr2[1][:, j * 128:(j + 1) * 128],
                                            qk[:, 1024 + j * 128:1024 + (j + 1) * 128], ident[:]),
                 r=["qk", "ident"], w=["PS:ptr2_1"])
        P.act(lambda e, g=g: e.activation(out=qTg[:, :, g * 128:(g + 1) * 128],
                                          in_=ptr2[0][:].rearrange("p (a t) -> p a t", t=128), func=AF.Copy),
              r=["PS:ptr2_0"], w=[("qTg", g)])
        P.dve(lambda e, g=g: e.tensor_copy(out=kTg[:, :, g * 128:(g + 1) * 128],
                                           in_=ptr2[1][:, 0:nkt * 128].rearrange("p (a t) -> p a t", t=128)),
              r=["PS:ptr2_1"], w=[("kTg", g)])
        P.dma("sp", v_d[:, :, t, :].rearrange("h p d -> p h d"), vt[t % 2][:],
              r=[f"vt{t % 2}"], is_output=True)
        ng = GT if t < NLT else NCT
        if g == ng - 1:
            tok0 = (t - g) * 128
            P.dma("sp", qT_d[:, :, tok0:tok0 + ng * 128].rearrange("a p t -> p a t"), qTg[:, :, 0:ng * 128],
                  r=[("qTg", i) for i in range(ng)], is_output=True)
            P.dma("sp", kT_d[:, :, tok0:tok0 + ng * 128].rearrange("a p t -> p a t"), kTg[:, :, 0:ng * 128],
                  r=[("kTg", i) for i in range(ng)], is_output=True)


def _fm(v):
    return np.ascontiguousarray(v.reshape(8, 128).T)


def layer_consts(mod, norm_g, i):
    ml, mc = mod[0, i].reshape(6, D), mod[1, i].reshape(6, D)
    vecs = [ml[0], ml[1], ml[3], ml[4], mc[0], mc[1], mc[3], mc[4]]
    modT = np.ascontiguousarray(np.stack([_fm(v) for v in vecs], axis=1)).astype(np.float32)
    gT = np.ascontiguousarray(np.stack([_fm(norm_g[i, j]) for j in range(4)], axis=1)).astype(np.float32)
    rows = np.ascontiguousarray(np.stack([ml[2], ml[5], mc[2], mc[5], norm_g[i, 1], norm_g[i, 3]], axis=0))
    return modT, gT, rows.astype(np.float32)


def rope_tables():
    half = 32
    freqs = (10000.0 ** (-np.arange(0, half, 2, dtype=np.float32) / half)).astype(np.float32)
    pos = np.arange(SEQ)
    rows = (pos // GRID_W).astype(np.float32)
    cols = (pos % GRID_W).astype(np.float32)
    ar = rows[:, None] * freqs
    ac = cols[:, None] * freqs
    cr, sr, cc, sc = np.cos(ar), np.sin(ar), np.cos(ac), np.sin(ac)
    C = np.concatenate([cr, cr, cc, cc], axis=1).astype(np.float32)
    S = np.concatenate([-sr, sr, -sc, sc], axis=1).astype(np.float32)
    return C, S


def run_pre(mix, x_all, ctx, wqkv, modT, gT):
    C, S = rope_tables()
    ident = np.eye(128, dtype=np.float32)
    in_maps = []
    for ci in range(NCORES):
        xs = np.concatenate([x_all[ci * TPC:(ci + 1) * TPC], ctx], axis=0)
        in_maps.append({"x": np.ascontiguousarray(xs), "wqkv": wqkv, "modT": modT, "gT": gT, "ident": ident,
                        "ropeC": np.ascontiguousarray(C[ci * TPC:(ci + 1) * TPC]),
                        "ropeS": np.ascontiguousarray(S[ci * TPC:(ci + 1) * TPC])})
    return _run(build_pre(mix), in_maps)


NKA = CTX + SEQ
NKC_A = NKA // 128


def build_att_A(lam_init, limit=None):
    def build(nc, P):
        qT_d = nc.dram_tensor("qT", [8, 128, TOKW], BF16, kind="ExternalInput").ap()
        kT_d = nc.dram_tensor("kT", [8, 128, NKA], BF16, kind="ExternalInput").ap()
        v_d = nc.dram_tensor("v", [8, 128, NKC_A, 129], BF16, kind="ExternalInput").ap()
        lam_d = nc.dram_tensor("lam", [128, 256], F32, kind="ExternalInput").ap()
        sub_d = nc.dram_tensor("subln", [128, 128], F32, kind="ExternalInput").ap()
        on_d = nc.dram_tensor("on", [TOKW, D], BF16, kind="ExternalOutput").ap()
        emit_att_A(nc, P, lam_init, qT_d, kT_d, v_d, lam_d, sub_d, on_d, limit)
    return build


def emit_att_A(nc, P, lam_init, qT_d, kloc, vloc, kall, vall, lam_d, sub_d, on_d, limit=None):
    nheads, nqc_lim, nkc_lim = limit if limit else (8, 5, NKC_A)

    def piece_of(kc):
        return 0 if kc < 2 else 1 + (kc - 2) // 16
    lam = _sb(P, "lam", [128, 256], F32)
    sub = _sb(P, "sub", [128, 128], F32)
    wsub = _sb(P, "wsub", [128, 128], F32)
    lj = _sb(P, "lamjunk", [128, 64], F32)
    ls = _sb(P, "lams", [128, 2], F32)
    le = _sb(P, "lame", [128, 2], F32)
    neglam = _sb(P, "neglam", [128, 1], F32)
    P.dma("sp", lam[:], lam_d, w=["lam"])
    P.dma("sp", sub[:], sub_d, w=["sub"])
    P.dve(lambda e: e.tensor_scalar(out=wsub[:], in0=sub[:], scalar1=float(1.0 - lam_init), scalar2=None,
                                    op0=ALU.mult), r=["sub"], w=["wsub"])
    for i in range(2):
        P.dve(lambda e, i=i: e.scalar_tensor_tensor(out=lj[:], in0=lam[:, 128 * i:128 * i + 64], scalar=1.0,
                                                   in1=lam[:, 128 * i + 64:128 * i + 128], op0=ALU.mult,
                                                   op1=ALU.mult, accum_out=ls[:, i:i + 1]),
              r=["lam"], w=["lj", ("ls", i)])
    P.act(lambda e: e.activation(out=le[:], in_=ls[:], func=AF.Exp), r=[("ls", 0), ("ls", 1)], w=["le"])
    P.dve(lambda e: e.tensor_tensor(out=neglam[:], in0=le[:, 1:2], in1=le[:, 0:1], op=ALU.subtract),
          r=["le"], w=["neglam0"])
    P.dve(lambda e: e.tensor_scalar(out=neglam[:], in0=neglam[:], scalar1=float(-lam_init), scalar2=None,
                                    op0=ALU.add), r=["neglam0"], w=["neglam"])
    Kt = [_sb(P, f"Kt{i}", [128, NKA], BF16) for i in range(2)]
    Vt = [_sb(P, f"Vt{i}", [128, NKC_A, 129], BF16) for i in range(2)]
    Qt = [_sb(P, f"Qt{i}", [128, TOKW], BF16) for i in range(2)]
    NPB = 3
    Pb = [[_sb(P, f"Pb{b}_{m}", [128, 512], BF16) for m in range(2)] for b in range(NPB)]
    Sb = [[_ps(P, f"S{b}_{m}", [128, 512], F32) for m in range(2)] for b in range(2)]
    Ob = [_ps(P, f"O{i}", [128, 512], F32) for i in range(3)]
    o1 = _sb(P, "o1", [128, 4, 128], F32)
    o2 = _sb(P, "o2", [128, 4, 128], F32)
    fj = _sb(P, "fjunk", [128, 128], F32)
    rs = _sb(P, "rs", [128, 4, 2], F32)
    r1 = _sb(P, "r1", [128, 4], F32)
    ssq = _sb(P, "assq", [128, 4], F32)
    lnv = _sb(P, "alnv", [128, 4], F32)
    rstd = _sb(P, "arstd", [128, 4], F32)
    ON = [_sb(P, f"ON{i}", [128, 4, 128], BF16) for i in range(2)]

    def oslot(m, j):
        s = m * 4 + j
        return s // 3, (s % 3) * 129

    nfin = 0
    for h in range(nheads):
        kt, vt, qt = Kt[h % 2], Vt[h % 2], Qt[h % 2]
        kk, vk, qk_ = f"Kt{h % 2}", f"Vt{h % 2}", f"Qt{h % 2}"
        P.dma("sp", kt[:, 0:CTX], kloc[h, :, TPC:TOKW], w=[(kk, 0)])
        P.dma("sp", vt[:, 0:NCT, :], vloc[h, :, NLT:NTT, :], w=[(vk, 0)])
        for rk_ in range(NCORES):
            P.dma("sp", kt[:, CTX + rk_ * TPC:CTX + (rk_ + 1) * TPC], kall[rk_, h * 128:(h + 1) * 128, 0:TPC],
                  w=[(kk, 1 + rk_)])
            P.dma("sp", vt[:, NCT + rk_ * NLT:NCT + (rk_ + 1) * NLT, :],
                  vall[rk_, h * 128:(h + 1) * 128, 0:NLT * 129].rearrange("p (c d) -> p c d", d=129),
                  w=[(vk, 1 + rk_)])
        P.dma("sp", qt[:], qT_d[h], w=[qk_])
        for qc in range(nqc_lim):
            is_ctx = qc == 4
            q0 = qc * 512
            nq = 256 if is_ctx else 512
            nj = nq // 128
            chunks = [0, 1] if is_ctx else list(range(min(NKC_A, nkc_lim)))
            nch = len(chunks)

            def stage_S(i):
                kc = chunks[i]
                for m in range(2):
                    sb_ = Sb[i % 2][m]
                    P.pe(lambda e, m=m, kc=kc, sb_=sb_, kt=kt, qt=qt, q0=q0, nq=nq: e.matmul(
                        sb_[:, 0:nq], kt[m * 64:(m + 1) * 64, kc * 128:(kc + 1) * 128],
                        qt[m * 64:(m + 1) * 64, q0:q0 + nq], start=True, stop=True),
                        r=[(kk, piece_of(kc)), qk_], w=[f"PS:S{i % 2}_{m}"])

            def stage_E(i):
                for m in range(2):
                    sb_ = Sb[i % 2][m]
                    pb_ = Pb[i % NPB][m]
                    P.act(lambda e, sb_=sb_, pb_=pb_, nq=nq: e.activation(out=pb_[:, 0:nq], in_=sb_[:, 0:nq], func=AF.Exp,
                                                                  scale=0.125),
                          r=[f"PS:S{i % 2}_{m}"], w=[f"Pb{i % NPB}_{m}"], embed=True)

            started = set()

            def stage_V(i, started=started):
                kc = chunks[i]
                for m in range(2):
                    pb_ = Pb[i % NPB][m]
                    for j in range(nj):
                        bank, off = oslot(m, j)
                        first_in_bank = (i == 0) and (bank not in started)
                        started.add(bank)
                        P.pe(lambda e, pb_=pb_, j=j, bank=bank, off=off, kc=kc, fb=first_in_bank, i=i, vt=vt, nch=nch: e.matmul(
                            Ob[bank][:, off:off + 129], pb_[:, j * 128:(j + 1) * 128], vt[:, kc, :],
                            start=fb, stop=(i == nch - 1), skip_group_check=True),
                            r=[f"Pb{i % NPB}_{m}", (vk, piece_of(kc))], w=[f"PS:O{bank}"])

            stage_S(0)
            if nch > 1:
                stage_S(1)
            stage_E(0)
            for i in range(nch):
                stage_V(i)
                if i + 2 < nch:
                    stage_S(i + 2)
                if i + 1 < nch:
                    stage_E(i + 1)
            on = ON[nfin % 2]
            onk = f"ON{nfin % 2}"
            nfin += 1
            for j in range(nj):
                b0, f0 = oslot(0, j)
                b1, f1 = oslot(1, j)
                P.dve(lambda e, j=j, b0=b0, f0=f0: e.reciprocal(out=rs[:, j, 0:1], in_=Ob[b0][:, f0 + 128:f0 + 129]),
                      r=[f"PS:O{b0}"], w=[("rs0", j)])
                P.dve(lambda e, j=j, b1=b1, f1=f1: e.reciprocal(out=rs[:, j, 1:2], in_=Ob[b1][:, f1 + 128:f1 + 129]),
                      r=[f"PS:O{b1}"], w=[("rs1", j)])
                P.dve(lambda e, j=j: e.tensor_tensor(out=r1[:, j:j + 1], in0=rs[:, j, 1:2], in1=neglam[:, 0:1],
                                                     op=ALU.mult), r=[("rs1", j), "neglam"], w=[("r1", j)])
                P.dve(lambda e, j=j, b0=b0, f0=f0: e.tensor_scalar(out=o1[:, j, :], in0=Ob[b0][:, f0:f0 + 128],
                                                                  scalar1=rs[:, j, 0:1], scalar2=None, op0=ALU.mult),
                      r=[f"PS:O{b0}", ("rs0", j)], w=[("o1", j)])
                P.dve(lambda e, j=j, b1=b1, f1=f1: e.scalar_tensor_tensor(
                    out=o2[:, j, :], in0=Ob[b1][:, f1:f1 + 128], scalar=r1[:, j:j + 1], in1=o1[:, j, :],
                    op0=ALU.mult, op1=ALU.add), r=[f"PS:O{b1}", ("r1", j), ("o1", j)], w=[("o2", j)])
            for j in range(nj):
                P.dve(lambda e, j=j: e.scalar_tensor_tensor(out=fj[:], in0=o2[:, j, :], scalar=1.0, in1=o2[:, j, :],
                                                            op0=ALU.mult, op1=ALU.mult, accum_out=ssq[:, j:j + 1]),
                      r=[("o2", j)], w=["fj", ("assq", j)])
            P.act(lambda e, nj=nj: e.activation(out=lnv[:, 0:nj], in_=ssq[:, 0:nj], func=AF.Ln, scale=1.0 / 128, bias=EPS),
                  r=[("assq", j) for j in range(nj)], w=["alnv"])
            P.act(lambda e, nj=nj: e.activation(out=rstd[:, 0:nj], in_=lnv[:, 0:nj], func=AF.Exp, scale=-0.5),
                  r=["alnv"], w=["arstd"])
            for j in range(nj):
                P.dve(lambda e, j=j, on=on: e.scalar_tensor_tensor(out=on[:, j, :], in0=o2[:, j, :],
                                                                  scalar=rstd[:, j:j + 1], in1=wsub[:],
                                                                  op0=ALU.mult, op1=ALU.mult),
                      r=[("o2", j), "arstd", "wsub"], w=[(onk, j)])
            P.dma("sp", on_d[q0:q0 + nq, h * 128:(h + 1) * 128].rearrange("(j p) d -> p j d", p=128),
                  on[:, 0:nj, :], r=[(onk, j) for j in range(nj)], is_output=True)


def gather_kv_A(pre_res):
    kT = np.concatenate([np.asarray(pre_res[0]["kT"])[:, :, TPC:]] +
                        [np.asarray(r["kT"])[:, :, :TPC] for r in pre_res], axis=2)
    v = np.concatenate([np.asarray(pre_res[0]["v"])[TPC:]] + [np.asarray(r["v"])[:TPC] for r in pre_res], axis=0)
    v = v.reshape(NKC_A, 128, 8, 129).transpose(2, 1, 0, 3)
    return np.ascontiguousarray(kT), np.ascontiguousarray(v)


def run_att_A(pre_res, a_lambda_j, a_subln_j, lam_init):
    kT, v = gather_kv_A(pre_res)
    lam = np.ascontiguousarray(np.broadcast_to(a_lambda_j.reshape(1, 256), (128, 256))).astype(np.float32)
    sub = np.ascontiguousarray(np.broadcast_to(a_subln_j.reshape(1, 128), (128, 128))).astype(np.float32)
    in_maps = [{"qT": np.asarray(pre_res[ci]["qT"]), "kT": kT, "v": v, "lam": lam, "subln": sub}
               for ci in range(NCORES)]
    return _run(build_att_A(lam_init), in_maps)


def emit_resid(P, yb, ykeys, xt, xkey, G, gkey, bufs, tag, ysb=None):
    junk, ssq2, ssq, lnv, rstd, tmp = bufs
    if ysb is not None:
        tmp = ysb
    tkey = (lambda c: (tag + "ysb", c)) if ysb is not None else (lambda c: (tag + "tmp", c))
    for c in range(2):
        P.act(lambda e, c=c: e.activation(out=junk[:, 0:512], in_=yb[c][:], func=AF.Square,
                                          accum_out=ssq2[:, c:c + 1]),
              r=[ykeys[c]], w=[tag + "junk", (tag + "ssq2", c)])
        if ysb is not None:
            P.dve(lambda e, c=c: e.tensor_copy(out=ysb[:, c * 512:(c + 1) * 512], in_=yb[c][:]),
                  r=[ykeys[c]], w=[(tag + "ysb", c)])
    P.dve(lambda e: e.tensor_tensor(out=ssq[:, 0:1], in0=ssq2[:, 0:1], in1=ssq2[:, 1:2], op=ALU.add),
          r=[(tag + "ssq2", 0), (tag + "ssq2", 1)], w=[tag + "ssq"])
    P.act(lambda e: e.activation(out=lnv[:, 0:1], in_=ssq[:, 0:1], func=AF.Ln, scale=1.0 / D, bias=EPS),
          r=[tag + "ssq"], w=[tag + "lnv"])
    P.act(lambda e: e.activation(out=rstd[:, 0:1], in_=lnv[:, 0:1], func=AF.Exp, scale=-0.5),
          r=[tag + "lnv"], w=[tag + "rstd"])
    for c in range(2):
        ysrc = yb[c][:] if ysb is None else ysb[:, c * 512:(c + 1) * 512]
        ykey = ykeys[c] if ysb is None else (tag + "ysb", c)
        P.dve(lambda e, c=c, ysrc=ysrc: e.scalar_tensor_tensor(out=tmp[:, c * 512:(c + 1) * 512], in0=ysrc,
                                                              scalar=rstd[:, 0:1], in1=G[:, c * 512:(c + 1) * 512],
                                                              op0=ALU.mult, op1=ALU.mult),
              r=[ykey, tag + "rstd", gkey], w=[tkey(c)])
    P.pool(lambda e, tmp=tmp: e.tensor_tensor(out=xt, in0=xt, in1=tmp[:], op=ALU.add),
           r=[tkey(0), tkey(1), xkey], w=[xkey])


def emit_gates(P, rows_d, which):
    gr = _sb(P, "gr", [128, D], F32)
    G = _sb(P, "G", [128, 2, D], F32)
    P.dma("sp", gr[:], rows_d[4 + which:5 + which, :].partition_broadcast(128), w=["gr"])
    for i in range(2):
        s = which + 2 * i
        P.dma("sp", G[:, i, :], rows_d[s:s + 1, :].partition_broadcast(128), w=[("G0", i)])
        P.dve(lambda e, i=i: e.tensor_tensor(out=G[:, i, :], in0=G[:, i, :], in1=gr[:], op=ALU.mult),
              r=[("G0", i), "gr"], w=[("G", i)])
    return G


def build_post():
    def build(nc, P):
        x_d = nc.dram_tensor("x", [TOKW, D], F32, kind="ExternalInput").ap()
        on_d = nc.dram_tensor("on", [TOKW, D], BF16, kind="ExternalInput").ap()
        wo_d = nc.dram_tensor("wo", [D, D], F32, kind="ExternalInput").ap()
        w13_d = nc.dram_tensor("w13", [D, 2 * FFN], F32, kind="ExternalInput").ap()
        w2_d = nc.dram_tensor("w2", [FFN, D], F32, kind="ExternalInput").ap()
        modT_d = nc.dram_tensor("modT", [128, 8, 8], F32, kind="ExternalInput").ap()
        gT_d = nc.dram_tensor("gT", [128, 4, 8], F32, kind="ExternalInput").ap()
        rows_d = nc.dram_tensor("rows", [6, D], F32, kind="ExternalInput").ap()
        id_d = nc.dram_tensor("ident", [128, 128], F32, kind="ExternalInput").ap()
        xm_d = nc.dram_tensor("xmid", [TOKW, D], F32, kind="Internal").ap()
        xo_d = nc.dram_tensor("xo", [TOKW, D], F32, kind="ExternalOutput").ap()
        emit_post(nc, P, x_d, on_d, wo_d, w13_d, w2_d, modT_d, gT_d, rows_d, id_d, xm_d, xo_d)
    return build


def emit_post(nc, P, x_d, on_d, wo_d, w13_d, w2_d, modT_d, gT_d, rows_d, id_d, xm_d, xo_d):
    outer = P.es
    with ExitStack() as es1:
        P.es = es1
        ident = _sb(P, "ident1", [128, 128], BF16)
        P.dma("pool", ident[:], id_d, w=["ident"])
        Wo = _sb(P, "Wo", [128, 8, D], BF16)
        for k in range(8):
            P.dma("pool", Wo[:, k, :], wo_d[k * 128:(k + 1) * 128, :], w=[("Wo", k)])
        Wok = [("Wo", k) for k in range(8)]
        G = emit_gates(P, rows_d, 0)
        xb = [_sb(P, f"pxb{i}", [128, D], F32) for i in range(2)]
        onb = [_sb(P, f"onb{i}", [128, D], BF16) for i in range(2)]
        onT = [_sb(P, f"onT{i}", [128, 8, 128], BF16) for i in range(2)]
        ptr = _ps(P, "p1tr", [128, D], BF16)
        yb = [[_ps(P, f"p1y{i}_{c}", [128, 512], F32) for c in range(2)] for i in range(2)]
        bufs2 = [(_sb(P, f"p1junk{i}", [128, 512], BF16), _sb(P, f"p1ssq2{i}", [128, 2], F32),
                  _sb(P, f"p1ssq{i}", [128, 1], F32), _sb(P, f"p1lnv{i}", [128, 1], F32),
                  _sb(P, f"p1rstd{i}", [128, 1], F32), _sb(P, f"p1tmp{i}", [128, D], F32)) for i in range(2)]
        for t in range(NTT):
            is_ctx = t >= NLT
            xt, xkey = xb[t % 2], f"pxb{t % 2}"
            ot, okey = onb[t % 2], f"onb{t % 2}"
            oT, oTkey = onT[t % 2], f"onT{t % 2}"
            P.dma("sp", xt[:], x_d[t * 128:(t + 1) * 128, :], w=[xkey])
            P.dma("sp", ot[:], on_d[t * 128:(t + 1) * 128, :], w=[okey])
            for fc in range(8):
                P.pe(lambda e, fc=fc, ot=ot: e.transpose(ptr[:, fc * 128:(fc + 1) * 128],
                                                        ot[:, fc * 128:(fc + 1) * 128], ident[:]),
                     r=[okey, "ident"], w=["PS:p1tr"])
            P.dve(lambda e, oT=oT: e.tensor_copy(out=oT[:].rearrange("p a t -> p (a t)"), in_=ptr[:]),
                  r=["PS:p1tr"], w=[oTkey])
            ybt = yb[t % 2]
            ykeys = [f"PS:p1y{t % 2}_{c}" for c in range(2)]
            for c in range(2):
                for k in range(8):
                    P.pe(lambda e, c=c, k=k, oT=oT, ybt=ybt: e.matmul(ybt[c][:], oT[:, k, :],
                                                                    Wo[:, k, c * 512:(c + 1) * 512],
                                                                    start=(k == 0), stop=(k == 7)),
                         r=[oTkey] + Wok, w=[ykeys[c]])
            gi = 1 if is_ctx else 0
            emit_resid(P, ybt, ykeys, xt[:], xkey, G[:, gi, :], ("G", gi), bufs2[t % 2], f"p1_{t % 2}")
            P.dma("sp", xm_d[t * 128:(t + 1) * 128, :], xt[:], r=[xkey], w=[("xm", t)])
        P.barrier()
        P.emit()
    with ExitStack() as es2:
        P.es = es2
        ident = _sb(P, "ident2", [128, 128], BF16)
        P.dma("pool", ident[:], id_d, w=["ident"])
        NF = FFN // 128
        W13 = _sb(P, "W13", [128, 8, 2 * FFN], BF16)
        W2 = _sb(P, "W2", [128, NF, D], BF16)
        for k in range(8):
            for hh in range(2):
                P.dma("pool", W13[:, k, hh * FFN:(hh + 1) * FFN], w13_d[k * 128:(k + 1) * 128, hh * FFN:(hh + 1) * FFN],
                      w=[("W13", k, hh)])
        W13k = [("W13", k, hh) for k in range(8) for hh in range(2)]
        for fc in range(NF):
            P.dma("pool", W2[:, fc, :], w2_d[fc * 128:(fc + 1) * 128, :], w=[("W2", fc)])
        W2k = [("W2", fc) for fc in range(NF)]
        A_l, B_l, A_c, B_c = emit_modsc(P, modT_d, gT_d, 1)
        G = emit_gates(P, rows_d, 1)
        GT = 4
        xg = _sb(P, "xg", [128, GT, D], F32)
        h2T = _sb(P, "h2T", [128, 8, GT * 128], BF16)
        hid = _sb(P, "hid", [128, NF, GT * 128], BF16)
        sg = [_sb(P, "sg0", [128, GT * 128], F32)] * 2
        ptr = _ps(P, "p2tr", [128, D], BF16)
        gb = [[_ps(P, f"p2g{i}_{c}", [128, 512], F32) for c in range(2)] for i in range(2)]
        yb = [_ps(P, f"p2y{c}", [128, 512], F32) for c in range(2)]
        nb = (_sb(P, "p2njunk", [128, D], BF16), _sb(P, "p2nssq", [128, 1], F32), _sb(P, "p2nlnv", [128, 1], F32),
              _sb(P, "p2nrstd", [128, 1], F32), _sb(P, "p2xn", [128, D], BF16), ptr)
        bufs2 = [(_sb(P, f"p2junk{i}", [128, 512], BF16), _sb(P, f"p2ssq2{i}", [128, 2], F32),
                  _sb(P, f"p2ssq{i}", [128, 1], F32), _sb(P, f"p2lnv{i}", [128, 1], F32),
                  _sb(P, f"p2rstd{i}", [128, 1], F32), None) for i in range(2)]
        ysb2 = [_sb(P, f"p2ysb{i}", [128, D], F32) for i in range(2)]
        ngroups = NLT // GT + 1
        for gidx in range(ngroups):
            is_ctx = gidx == ngroups - 1
            ng = NCT if is_ctx else GT
            ntok = ng * 128
            t0 = gidx * GT
            for j in range(ng):
                t = t0 + j
                P.dma("sp", xg[:, j, :], xm_d[t * 128:(t + 1) * 128, :], r=[("xm", t)], w=[("xg", j)])
                hview = h2T[:, :, j * 128:(j + 1) * 128]
                emit_norm_T(P, xg[:, j, :], ("xg", j), hview, ("h2T", j), A_c if is_ctx else A_l,
                            B_c if is_ctx else B_l, ident, nb, "p2n")
            hkeys = [(("h2T", j), fc) for j in range(ng) for fc in range(8)]
            for fc in range(NF):
                gbt = gb[fc % 2]
                gk = [f"PS:p2g{fc % 2}_{c}" for c in range(2)]
                for c in range(2):
                    col0 = c * FFN + fc * 128
                    for k in range(8):
                        P.pe(lambda e, c=c, k=k, col0=col0, gbt=gbt, ntok=ntok: e.matmul(
                            gbt[c][:, 0:ntok], W13[:, k, col0:col0 + 128], h2T[:, k, 0:ntok],
                            start=(k == 0), stop=(k == 7)), r=hkeys + W13k, w=[gk[c]])
                sgt = sg[fc % 2]
                P.act(lambda e, gbt=gbt, sgt=sgt, ntok=ntok: e.activation(out=sgt[:, 0:ntok], in_=gbt[0][:, 0:ntok],
                                                                         func=AF.Silu),
                      r=[gk[0]], w=["sg0"])
                P.dve(lambda e, gbt=gbt, sgt=sgt, fc=fc, ntok=ntok: e.tensor_tensor(
                    out=hid[:, fc, 0:ntok], in0=gbt[1][:, 0:ntok], in1=sgt[:, 0:ntok], op=ALU.mult),
                    r=[gk[1], "sg0"], w=[("hid", fc)])
            hidk = [("hid", fc) for fc in range(NF)]
            for j in range(ng):
                t = t0 + j
                ykeys = [f"PS:p2y{c}" for c in range(2)]
                for c in range(2):
                    for fc in range(NF):
                        P.pe(lambda e, c=c, fc=fc, j=j: e.matmul(yb[c][:], hid[:, fc, j * 128:(j + 1) * 128],
                                                               W2[:, fc, c * 512:(c + 1) * 512],
                                                               start=(fc == 0), stop=(fc == NF - 1)),
                             r=hidk + W2k, w=[ykeys[c]])
                gi = 1 if is_ctx else 0
                emit_resid(P, yb, ykeys, xg[:, j, :], ("xg", j), G[:, gi, :], ("G", gi), bufs2[j % 2], f"p2_{j % 2}",
                           ysb=ysb2[j % 2])
                P.dma("sp", xo_d[t * 128:(t + 1) * 128, :], xg[:, j, :], r=[("xg", j)], is_output=True)
        P.barrier()
        P.emit()
    P.es = outer


def run_post(x_all, ctx, on_res, wo, w13, w2, modT, gT, rows):
    ident = np.eye(128, dtype=np.float32)
    in_maps = []
    for ci in range(NCORES):
        xs = np.concatenate([x_all[ci * TPC:(ci + 1) * TPC], ctx], axis=0)
        in_maps.append({"x": np.ascontiguousarray(xs), "on": np.asarray(on_res[ci]["on"]), "wo": wo, "w13": w13,
                        "w2": w2, "modT": modT, "gT": gT, "rows": rows, "ident": ident})
    res = _run(build_post(), in_maps)
    x_new = np.concatenate([r["xo"][:TPC] for r in res], axis=0)
    ctx_new = res[0]["xo"][TPC:]
    return x_new, ctx_new


def emit_local_units(P, units, tagp=""):
    Sb = [_ps(P, f"lS{i}", [128, 512], F32) for i in range(4)]
    Ob = [_ps(P, f"lO{i}", [128, 512], F32) for i in range(2)]
    NPB = 3
    Pb = [[_sb(P, f"lPb{i}_{e}", [128, 512], BF16) for e in range(2)] for i in range(NPB)]
    Pt = [_sb(P, f"lPt{i}", [128, 512], BF16) for i in range(2)]
    Tf = [_sb(P, f"lTf{i}", [128, 512], F32) for i in range(2)]
    flat = []
    for ui, u in enumerate(units):
        for ii, it in enumerate(u["items"]):
            cnt = [0, 0]
            pos = []
            for sub in it["subs"]:
                e_ = sub[4]
                pos.append(cnt[e_])
                cnt[e_] += 1
            it["pos"] = pos
            it["cnt"] = cnt
            flat.append((ui, ii, it))
    n = len(flat)

    def stage_S(i):
        ui, ii, it = flat[i]
        if ii == 0 and units[ui].get("pre") is not None:
            units[ui]["pre"]()
        for si, (kap, qap, vap, og, e_) in enumerate(it["subs"]):
            bk = 2 * (i % 2) + e_
            sb_ = Sb[bk]
            ps_ = it["pos"][si]
            P.pe(lambda e, sb_=sb_, ps_=ps_, kap=kap, qap=qap: e.matmul(sb_[:, ps_ * 128:(ps_ + 1) * 128], kap, qap,
                                                                     start=True, stop=True, skip_group_check=True),
                 r=it["rk"], w=[f"PS:lS{bk}"])

    def stage_E(i):
        ui, ii, it = flat[i]
        for e_ in range(2):
            if it["cnt"][e_] == 0:
                continue
            bk = 2 * (i % 2) + e_
            sb_ = Sb[bk]
            pb_ = Pb[i % NPB][e_]
            w_ = it["cnt"][e_] * 128
            skey = f"PS:lS{bk}"
            pkey = f"lPb{i % NPB}_{e_}"
            if it["kind"] == "plain":
                P.act(lambda e, sb_=sb_, pb_=pb_, w_=w_: e.activation(out=pb_[:, 0:w_], in_=sb_[:, 0:w_], func=AF.Exp,
                                                                     scale=0.125), r=[skey], w=[pkey])
            elif it["kind"] == "mask":
                pt_ = Pt[e_]
                aux = it["aux"][e_]
                P.act(lambda e, sb_=sb_, pt_=pt_, w_=w_: e.activation(out=pt_[:, 0:w_], in_=sb_[:, 0:w_], func=AF.Exp,
                                                                     scale=0.125), r=[skey], w=[f"lPt{e_}"])
                P.dve(lambda e, pb_=pb_, pt_=pt_, aux=aux, w_=w_: e.tensor_tensor(
                    out=pb_[:, 0:w_].rearrange("p (g q) -> p g q", q=128),
                    in0=pt_[:, 0:w_].rearrange("p (g q) -> p g q", q=128), in1=aux, op=ALU.mult),
                    r=[f"lPt{e_}"] + it["auxk"], w=[pkey])
            else:
                tf_ = Tf[i % 2]
                aux = it["aux"]
                P.dve(lambda e, sb_=sb_, tf_=tf_, aux=aux, w_=w_: e.scalar_tensor_tensor(
                    out=tf_[:, 0:w_], in0=sb_[:, 0:w_], scalar=0.125, in1=aux, op0=ALU.mult, op1=ALU.add),
                    r=[skey] + it["auxk"], w=[f"lTf{i % 2}"])
                P.act(lambda e, tf_=tf_, pb_=pb_, w_=w_: e.activation(out=pb_[:, 0:w_], in_=tf_[:, 0:w_], func=AF.Exp),
                      r=[f"lTf{i % 2}"], w=[pkey])

    started = {}

    def stage_V(i):
        ui, ii, it = flat[i]
        u = units[ui]
        ob_ = Ob[ui % 2]
        nit = len(u["items"])
        for si, (kap, qap, vap, og, e_) in enumerate(it["subs"]):
            pb_ = Pb[i % NPB][e_]
            ps_ = it["pos"][si]
            first = started.get(ui) is None
            started[ui] = True
            last = (ii == nit - 1) and (si == len(it["subs"]) - 1)
            P.pe(lambda e, pb_=pb_, ob_=ob_, ps_=ps_, vap=vap, og=og, first=first, last=last: e.matmul(
                ob_[:, og * 65:(og + 1) * 65], pb_[:, ps_ * 128:(ps_ + 1) * 128], vap, start=first, stop=last,
                skip_group_check=True), r=[f"lPb{i % NPB}_{e_}"] + it["vk"], w=[f"PS:lO{ui % 2}"])
        if ii == nit - 1:
            u["fin"](u, ob_, f"PS:lO{ui % 2}")

    if n == 0:
        return
    stage_S(0)
    if n > 1:
        stage_S(1)
    stage_E(0)
    for i in range(n):
        stage_V(i)
        if i + 2 < n:
            stage_S(i + 2)
        if i + 1 < n:
            stage_E(i + 1)


def make_fin(P, G, ONt, onkey_fn, col0_fn, esink=None):
    den = _sb(P, "fden", [128, 4], F32)
    rr = _sb(P, "frr", [128, 4], F32)

    def fin(u, ob_, okey):
        ov = ob_[:, 0:G * 65].rearrange("p (g d) -> p g d", d=65)
        col0 = col0_fn(u)
        dst = ONt[:, u["blk"], col0:col0 + G * 64].rearrange("p (g d) -> p g d", d=64)
        if esink is not None:
            h0 = u["h0"]
            P.dve(lambda e: e.tensor_tensor(out=den[:, 0:G], in0=ov[:, :, 64], in1=esink[:, h0:h0 + G], op=ALU.add),
                  r=[okey, "esink"], w=["fden"])
            P.dve(lambda e: e.reciprocal(out=rr[:, 0:G], in_=den[:, 0:G]), r=["fden"], w=["frr"])
        else:
            P.dve(lambda e: e.reciprocal(out=rr[:, 0:G], in_=ov[:, :, 64]), r=[okey], w=["frr"])
        P.dve(lambda e: e.tensor_tensor(out=dst, in0=ov[:, :, 0:64],
                                        in1=rr[:, 0:G].unsqueeze(2).broadcast_to([128, G, 64]), op=ALU.mult),
              r=[okey, "frr"], w=[onkey_fn(u)])
    return fin


NKL_C = CTX + 128 + TPC + 128
NKC_C = NKL_C // 128


def build_att_C():
    def build(nc, P):
        qT_d = nc.dram_tensor("qT", [8, 128, TOKW], BF16, kind="ExternalInput").ap()
        kT_d = nc.dram_tensor("kT", [4, 128, NKL_C], BF16, kind="ExternalInput").ap()
        v_d = nc.dram_tensor("v", [128, NKC_C, 4 * 65], BF16, kind="ExternalInput").ap()
        mk_d = nc.dram_tensor("masks", [128, 4, 128], F32, kind="ExternalInput").ap()
        sk_d = nc.dram_tensor("sink", [128, 16], F32, kind="ExternalInput").ap()
        on_d = nc.dram_tensor("on", [TOKW, D], BF16, kind="ExternalOutput").ap()
        emit_att_C(nc, P, qT_d, kT_d, v_d, mk_d, sk_d, on_d)
    return build


def emit_att_C(nc, P, qT_d, kT_d, v_d, mk_d, sk_d, on_d):
    Q = _sb(P, "cQ", [128, 8, TOKW], BF16)
    Kc = _sb(P, "cK", [128, 4, NKL_C], BF16)
    V = _sb(P, "cV", [128, NKC_C, 4 * 65], BF16)
    mk = _sb(P, "cmk", [128, 4, 128], BF16)
    sk = _sb(P, "csk", [128, 16], F32)
    esink = _sb(P, "cesink", [128, 16], F32)
    ONt = _sb(P, "cON", [128, NTT, D], BF16)
    for p in range(8):
        P.dma("sp", Q[:, p, :], qT_d[p], w=[("cQ", p)])
    for kv in range(4):
        P.dma("sp", Kc[:, kv, :], kT_d[kv], w=[("cK", kv)])
    for kv in range(4):
        P.dma("sp", V[:, :, kv * 65:(kv + 1) * 65], v_d[kv], w=[("cV", kv)])
    P.dma("pool", mk[:], mk_d, w=["cmk"])
    P.dma("sp", sk[:], sk_d, w=["csk"])
    P.act(lambda e: e.activation(out=esink[:], in_=sk[:], func=AF.Exp), r=["csk"], w=["esink"])
    fin = make_fin(P, 4, ONt, lambda u: ("cON", u["blk"], u["kv"]), lambda u: u["kv"] * 256, esink)
    units = []
    for blk in range(NTT):
        is_ctx = blk >= NLT
        for kv in range(4):
            q0 = blk * 128
            chunks = [(0, "plain", None), (1, "plain", None)]
            if not is_ctx:
                chunks.append((2 + blk, "mask", 2 if blk == 0 else 0))
                chunks.append((3 + blk, "plain", None))
                chunks.append((4 + blk, "mask", 3 if blk == NLT - 1 else 1))
            items = []
            for (lc, kind, mi) in chunks:
                subs = []
                for g in range(4):
                    e_ = g % 2
                    pr = 2 * kv + g // 2
                    subs.append((Kc[e_ * 64:(e_ + 1) * 64, kv, lc * 128:(lc + 1) * 128],
                                 Q[e_ * 64:(e_ + 1) * 64, pr, q0:q0 + 128],
                                 V[:, lc, kv * 65:(kv + 1) * 65], g, e_))
                it = dict(subs=subs, kind=kind, rk=[("cK", kv), ("cQ", 2 * kv), ("cQ", 2 * kv + 1)], vk=[("cV", kv)])
                if kind == "mask":
                    mb = mk[:, mi, :].unsqueeze(1).broadcast_to([128, 2, 128])
                    it["aux"] = [mb, mb]
                    it["auxk"] = ["cmk"]
                items.append(it)
            units.append(dict(items=items, G=4, fin=fin, blk=blk, kv=kv, h0=4 * kv))
    for u in units:
        for it in u["items"]:
            if it["kind"] == "mask":
                it["mask3"] = True
    emit_local_units(P, units)
    for blk in range(NTT):
        P.dma("sp", on_d[blk * 128:(blk + 1) * 128, :], ONt[:, blk, :],
              r=[("cON", blk, kv) for kv in range(4)], is_output=True)


def run_att_C(pre_res, c_sink_j):
    kT_all = np.concatenate([np.asarray(r["kT"])[:, :, :TPC] for r in pre_res], axis=2)
    kT_ctx = np.asarray(pre_res[0]["kT"])[:, :, TPC:]
    v_all = np.concatenate([np.asarray(r["v"])[:TPC] for r in pre_res], axis=0)
    v_ctx = np.asarray(pre_res[0]["v"])[TPC:]
    zk = np.zeros((4, 128, 128), dtype=kT_all.dtype)
    zv = np.zeros((128, v_all.shape[1]), dtype=v_all.dtype)
    p = np.arange(128)
    m_lo = (p[:, None] >= p[None, :]).astype(np.float32)
    m_hi = (p[:, None] <= p[None, :]).astype(np.float32)
    sink = np.ascontiguousarray(np.broadcast_to(c_sink_j.reshape(1, 16), (128, 16))).astype(np.float32)
    in_maps = []
    for ci in range(NCORES):
        lo, hi = ci * TPC, (ci + 1) * TPC
        kb = kT_all[:, :, lo - 128:lo] if ci > 0 else zk
        ka = kT_all[:, :, hi:hi + 128] if ci < NCORES - 1 else zk
        vb = v_all[lo - 128:lo] if ci > 0 else zv
        va = v_all[hi:hi + 128] if ci < NCORES - 1 else zv
        kT = np.concatenate([kT_ctx, kb, kT_all[:, :, lo:hi], ka], axis=2)
        v = np.concatenate([v_ctx, vb, v_all[lo:hi], va], axis=0)
        v = v.reshape(NKC_C, 128, v.shape[1]).transpose(1, 0, 2)
        masks = np.stack([m_lo, m_hi, m_lo if ci > 0 else np.zeros_like(m_lo),
                          m_hi if ci < NCORES - 1 else np.zeros_like(m_hi)], axis=1)
        in_maps.append({"qT": np.asarray(pre_res[ci]["qT"]), "kT": np.ascontiguousarray(kT),
                        "v": np.ascontiguousarray(v), "masks": np.ascontiguousarray(masks).astype(np.float32),
                        "sink": sink})
    return _run(build_att_C(), in_maps)


NKL_B = CTX + 256 + TPC + 256
NKC_B = NKL_B // 128
NA_CLASSES = [(2, -2, 5), (0, -2, 6), (1, -2, 5), (14, -2, 5), (15, -3, 6)]


def na_class(blk):
    return {0: 1, 1: 2, 14: 3, 15: 4}.get(blk, 0)


def na_tables(rpb, core):
    out = np.full((16, 128, 5, 6, 128), -30000.0, dtype=np.float32)
    q = np.arange(128)
    p = np.arange(128)
    for cl, (b, dstart, ns) in enumerate(NA_CLASSES):
        gb = 16 * core + b
        qr = 2 * gb + q // 64
        qc = q % 64
        kr0 = np.clip(qr - 4, 0, 256 - 8)
        kc0 = np.clip(qc - 8, 0, 64 - 16)
        for s in range(ns):
            gc = gb + dstart + s
            if gc < 0 or gc > 127:
                continue
            kr = 2 * gc + p // 64
            kc = p % 64
            valid = ((kr[:, None] >= kr0[None, :]) & (kr[:, None] < kr0[None, :] + 8) &
                     (kc[:, None] >= kc0[None, :]) & (kc[:, None] < kc0[None, :] + 16))
            dr = np.clip(kr[:, None] - qr[None, :] + 7, 0, 14)
            dc = np.clip(kc[:, None] - qc[None, :] + 15, 0, 30)
            vals = rpb[:, dr, dc]
            out[:, :, cl, s, :] = np.where(valid[None], vals, np.float32(-30000.0))
    return out


def build_att_B():
    def build(nc, P):
        qT_d = nc.dram_tensor("qT", [8, 128, TOKW], BF16, kind="ExternalInput").ap()
        kT_d = nc.dram_tensor("kT", [8, 128, NKL_B], BF16, kind="ExternalInput").ap()
        v_d = nc.dram_tensor("v", [16, 128, NKC_B, 65], BF16, kind="ExternalInput").ap()
        b_d = nc.dram_tensor("bias", [16, 128, 5 * 6 * 128], F32, kind="ExternalInput").ap()
        on_d = nc.dram_tensor("on", [TOKW, D], BF16, kind="ExternalOutput").ap()
        emit_att_B(nc, P, qT_d, kT_d, v_d, b_d, on_d)
    return build


def emit_att_B(nc, P, qT_d, kT_d, v_d, b_d, on_d):
    Q = [_sb(P, f"bQ{i}", [128, TOKW], BF16) for i in range(2)]
    Kb = [_sb(P, f"bK{i}", [128, NKL_B], BF16) for i in range(2)]
    V = [_sb(P, f"bV{i}", [128, NKC_B, 65], BF16) for i in range(2)]
    Bt = [_sb(P, f"bB{i}", [128, 5, 6, 128], F32) for i in range(2)]
    ONt = _sb(P, "bON", [128, NTT, D], BF16)

    def load_head(h):
        pr, e_ = h // 2, h % 2
        if e_ == 0:
            P.dma("sp", Q[pr % 2][:], qT_d[pr], w=[f"bQ{pr % 2}"])
            P.dma("sp", Kb[pr % 2][:], kT_d[pr], w=[f"bK{pr % 2}"])
        P.dma("sp", V[h % 2][:], v_d[h], w=[f"bV{h % 2}"])
        P.dma("sp", Bt[h % 2][:].rearrange("p c s q -> p (c s q)"), b_d[h], w=[f"bB{h % 2}"])

    fin = make_fin(P, 1, ONt, lambda u: ("bON", u["blk"], u["h"]), lambda u: u["h"] * 64, None)
    units = []
    load_head(0)
    for h in range(16):
        pr, e_ = h // 2, h % 2
        Qh, Kh, Vh, Bh = Q[pr % 2], Kb[pr % 2], V[h % 2], Bt[h % 2]
        rk = [f"bQ{pr % 2}", f"bK{pr % 2}"]
        for blk in range(NTT):
            is_ctx = blk >= NLT
            q0 = blk * 128

            def sub(lc):
                return (Kh[e_ * 64:(e_ + 1) * 64, lc * 128:(lc + 1) * 128], Qh[e_ * 64:(e_ + 1) * 64, q0:q0 + 128],
                        Vh[:, lc, :], 0, e_)

            items = [dict(subs=[sub(0), sub(1)], kind="plain", rk=rk, vk=[f"bV{h % 2}"])]
            if not is_ctx:
                cl = na_class(blk)
                _, dstart, ns = NA_CLASSES[cl]
                for s0 in range(0, ns, 4):
                    s1 = min(ns, s0 + 4)
                    subs = [sub(4 + blk + dstart + s) for s in range(s0, s1)]
                    items.append(dict(subs=subs, kind="bias", rk=rk, vk=[f"bV{h % 2}"],
                                      aux=Bh[:, cl, s0:s1, :].rearrange("p s q -> p (s q)"),
                                      auxk=[f"bB{h % 2}"]))
            u = dict(items=items, G=1, fin=fin, blk=blk, h=h)
            if blk == 2 and h + 1 < 16:
                u["pre"] = (lambda hn=h + 1: load_head(hn))
            units.append(u)
    emit_local_units(P, units)
    for blk in range(NTT):
        P.dma("sp", on_d[blk * 128:(blk + 1) * 128, :], ONt[:, blk, :],
              r=[("bON", blk, h) for h in range(16)], is_output=True)


def run_att_B(pre_res, rpb_j):
    kT_all = np.concatenate([np.asarray(r["kT"])[:, :, :TPC] for r in pre_res], axis=2)
    kT_ctx = np.asarray(pre_res[0]["kT"])[:, :, TPC:]
    v_all = np.concatenate([np.asarray(r["v"])[:TPC] for r in pre_res], axis=0)
    v_ctx = np.asarray(pre_res[0]["v"])[TPC:]
    zk = np.zeros((8, 128, 256), dtype=kT_all.dtype)
    zv = np.zeros((256, v_all.shape[1]), dtype=v_all.dtype)
    in_maps = []
    for ci in range(NCORES):
        lo, hi = ci * TPC, (ci + 1) * TPC
        kb = kT_all[:, :, lo - 256:lo] if ci > 0 else zk
        ka = kT_all[:, :, hi:hi + 256] if ci < NCORES - 1 else zk
        vb = v_all[lo - 256:lo] if ci > 0 else zv
        va = v_all[hi:hi + 256] if ci < NCORES - 1 else zv
        kT = np.concatenate([kT_ctx, kb, kT_all[:, :, lo:hi], ka], axis=2)
        v = np.concatenate([v_ctx, vb, v_all[lo:hi], va], axis=0)
        v = v.reshape(NKC_B, 128, 16, 65).transpose(2, 1, 0, 3)
        bias = na_tables(rpb_j, ci).reshape(16, 128, 5 * 6 * 128)
        in_maps.append({"qT": np.asarray(pre_res[ci]["qT"]), "kT": np.ascontiguousarray(kT),
                        "v": np.ascontiguousarray(v), "bias": np.ascontiguousarray(bias)})
    return _run(build_att_B(), in_maps)


LAYER_MIX = ["A", "B", "C", "A"]
MOD_VECS = [(0, 0), (1, 0), (3, 0), (4, 0), (0, 1), (1, 1), (3, 1), (4, 1)]


def _phase_end(P):
    P.barrier()
    P.emit()


def emit_mod_phase(nc, P, cT, wblk, bblk, mod_loc, mod_all):
    outer = P.es
    with ExitStack() as es:
        P.es = es
        s_raw = _sb(P, "s_raw", [128, 16], F32)
        s = _sb(P, "s_act", [128, 16], F32)
        bt = _sb(P, "bt", [2, MOD_BLK * 512], F32)
        ot = _sb(P, "ot", [2, MOD_BLK * 512], F32)
        wt = [_sb(P, f"wt{i}", [128, 8, 512], F32) for i in range(2)]
        pst = [_ps(P, f"mps{i}", [128, 512], F32) for i in range(2)]
        P.dma("sp", s_raw[:], cT, w=["s_raw"])
        P.dma("sp", bt[:], bblk, w=["bt"])
        P.act(lambda e: e.activation(out=s[:], in_=s_raw[:], func=AF.Silu), r=["s_raw"], w=["s"])
        for b in range(MOD_BLK):
            w_ = wt[b % 2]
            pp = pst[b % 2]
            P.dma("sp", w_[:], wblk[b].rearrange("(k p) n -> p k n", p=128), w=[f"wt{b % 2}"])
            for k in range(8):
                P.pe(lambda e, k=k, w_=w_, pp=pp: e.matmul(pp[0:2, :], s[:, 2 * k:2 * k + 2], w_[:, k, :],
                                                          start=(k == 0), stop=(k == 7)),
                     r=["s", f"wt{b % 2}"], w=[f"PS:mps{b % 2}"])
            P.dve(lambda e, b=b, pp=pp: e.tensor_tensor(out=ot[:, b * 512:(b + 1) * 512], in0=pp[0:2, :],
                                                       in1=bt[:, b * 512:(b + 1) * 512], op=ALU.add),
                  r=[f"PS:mps{b % 2}", "bt"], w=[("ot", b)])
        P.dma("sp", mod_loc, ot[:], r=[("ot", b) for b in range(MOD_BLK)], w=["mod_loc"])
        P.cc(lambda e: e.collective_compute("AllGather", ALU.bypass, replica_groups=[list(range(NCORES))],
                                            ins=[mod_loc.opt()], outs=[mod_all.opt()]),
             r=["mod_loc"], w=["mod_all"])
        _phase_end(P)
    P.es = outer


def emit_consts(nc, P, i, mod_all, grow, modT_i, rows_i):
    def vec(ch, rr):
        row = (2 * i + ch // 3) * 2 + rr
        c0 = (ch % 3) * 1024
        return mod_all[row:row + 1, c0:c0 + 1024]
    for v, (ch, rr) in enumerate(MOD_VECS):
        P.dma("sp", modT_i[:, v, :], vec(ch, rr).rearrange("o (k p) -> p (o k)", p=128), w=[("modT_i", v)],
              allow_slow_non_contiguous=True)
    for ri, (ch, rr) in enumerate([(2, 0), (5, 0), (2, 1), (5, 1)]):
        P.dma("sp", rows_i[ri:ri + 1, :], vec(ch, rr), w=[("rows_i", ri)])
    for g in range(2):
        P.dma("sp", rows_i[4 + g:5 + g, :], grow[i, g:g + 1, :], w=[("rows_i", 4 + g)])
    _phase_end(P)


def emit_window(nc, P, mix, kloc, vloc, kpack, vpack, kpall, vpall, sel_d, kwin, vwin):
    cfg = MIX[mix]
    nkp, vh, dv = cfg["nkp"], cfg["vh"], cfg["dv"]
    HT = 2 if mix == "B" else 1
    HB = HT * 128
    outer = P.es
    with ExitStack() as es:
        P.es = es
        P.dma("sp", kpack[:, :, 0:HB], kloc[:, :, 0:HB], w=["kpack0"])
        P.dma("sp", kpack[:, :, HB:2 * HB], kloc[:, :, TPC - HB:TPC], w=["kpack1"])
        P.dma("sp", vpack[:, :, 0:HT, :], vloc[:, :, 0:HT, :], w=["vpack0"])
        P.dma("sp", vpack[:, :, HT:2 * HT, :], vloc[:, :, NLT - HT:NLT, :], w=["vpack1"])
        P.cc(lambda e: e.collective_compute("AllGather", ALU.bypass, replica_groups=[list(range(NCORES))],
                                            ins=[kpack.rearrange("a p t -> (a p) t").opt()],
                                            outs=[kpall.rearrange("r a p t -> (r a p) t").opt()]),
             r=["kpack0", "kpack1"], w=["kpall"])
        P.cc(lambda e: e.collective_compute("AllGather", ALU.bypass, replica_groups=[list(range(NCORES))],
                                            ins=[vpack.rearrange("h p c d -> (h p) (c d)").opt()],
                                            outs=[vpall.rearrange("r h p c d -> (r h p) (c d)").opt()]),
             r=["vpack0", "vpack1"], w=["vpall"])
        P.dma("sp", kwin[:, :, 0:CTX], kloc[:, :, TPC:TOKW], w=["kwin_c"])
        P.dma("sp", kwin[:, :, CTX + HB:CTX + HB + TPC], kloc[:, :, 0:TPC], w=["kwin_o"])
        P.dma("sp", vwin[:, :, 0:NCT, :], vloc[:, :, NLT:NTT, :], w=["vwin_c"])
        P.dma("sp", vwin[:, :, NCT + HT:NCT + HT + NLT, :], vloc[:, :, 0:NLT, :], w=["vwin_o"])
        sel = _sb(P, "sel", [128, 16], F32)
        P.dma("sp", sel[:], sel_d, w=["sel"])
        KP = _sb(P, "KP", [128, NCORES, nkp, 2 * HB], BF16)
        VP = _sb(P, "VP", [128, NCORES, vh, 2 * HT * (dv + 1)], BF16)
        for r in range(NCORES):
            P.dma("sp", KP[:, r, :, :], kpall[r].rearrange("a p t -> p a t"), r=["kpall"], w=[("KP", r)])
            P.dma("sp", VP[:, r, :, :], vpall[r].rearrange("h p c d -> p h (c d)"), r=["vpall"], w=[("VP", r)])
        hk = _sb(P, "hk", [128, 2, nkp, HB], BF16)
        hv = _sb(P, "hv", [128, 2, vh, HT * (dv + 1)], BF16)
        P.dve(lambda e: e.memset(hk[:], 0.0), w=["hk"])
        P.dve(lambda e: e.memset(hv[:], 0.0), w=["hv"])
        W1 = HT * (dv + 1)
        for side in range(2):
            for r in range(NCORES):
                so = HB if side == 0 else 0
                sv = W1 if side == 0 else 0
                P.dve(lambda e, side=side, r=r, so=so: e.scalar_tensor_tensor(
                    out=hk[:, side, :, :], in0=KP[:, r, :, so:so + HB], scalar=sel[:, side * 8 + r:side * 8 + r + 1],
                    in1=hk[:, side, :, :], op0=ALU.mult, op1=ALU.add), r=[("KP", r), "sel", "hk"], w=["hk"])
                P.dve(lambda e, side=side, r=r, sv=sv: e.scalar_tensor_tensor(
                    out=hv[:, side, :, :], in0=VP[:, r, :, sv:sv + W1], scalar=sel[:, side * 8 + r:side * 8 + r + 1],
                    in1=hv[:, side, :, :], op0=ALU.mult, op1=ALU.add), r=[("VP", r), "sel", "hv"], w=["hv"])
        for side in range(2):
            t0 = CTX if side == 0 else CTX + HB + TPC
            c0 = NCT if side == 0 else NCT + HT + NLT
            P.dma("sp", kwin[:, :, t0:t0 + HB].rearrange("a p t -> p a t"), hk[:, side, :, :], r=["hk"],
                  w=[("kwin_h", side)])
            P.dma("sp", vwin[:, :, c0:c0 + HT, :].rearrange("h p c d -> p h (c d)"), hv[:, side, :, :], r=["hv"],
                  w=[("vwin_h", side)])
        _phase_end(P)
    P.es = outer


def build_fused(nc, P):
    dt_in = lambda name, shape, dt=F32: nc.dram_tensor(name, shape, dt, kind="ExternalInput").ap()
    dt_i = lambda name, shape, dt=F32: nc.dram_tensor(name, shape, dt, kind="Internal").ap()
    xin = dt_in("xin", [TOKW, D])
    cT = dt_in("cT", [128, 16])
    wblk = dt_in("wblk", [MOD_BLK, D, 512])
    bblk = dt_in("bblk", [2, MOD_BLK * 512])
    gT_all = dt_in("gT_all", [DEPTH, 128, 4, 8])
    grow = dt_in("grow", [DEPTH, 2, D])
    id_d = dt_in("ident", [128, 128])
    rc_d = dt_in("ropeC", [TPC, 64])
    rs_d = dt_in("ropeS", [TPC, 64])
    wqkv = {0: dt_in("wqkv0", [D, 3072]), 1: dt_in("wqkv1", [D, 3072]), 2: dt_in("wqkv2", [D, 1536]),
            3: dt_in("wqkv3", [D, 3072])}
    wo = [dt_in(f"wo{i}", [D, D]) for i in range(DEPTH)]
    w13 = [dt_in(f"w13_{i}", [D, 2 * FFN]) for i in range(DEPTH)]
    w2 = [dt_in(f"w2_{i}", [FFN, D]) for i in range(DEPTH)]
    lam_d = [dt_in("lam0", [128, 256]), dt_in("lam1", [128, 256])]
    sub_d = [dt_in("sub0", [128, 128]), dt_in("sub1", [128, 128])]
    bias_d = dt_in("nabias", [16, 128, 5 * 6 * 128])
    mk_d = dt_in("cmasks", [128, 4, 128])
    sk_d = dt_in("csink", [128, 16])
    selB = dt_in("sel", [128, 16])
    y_d = nc.dram_tensor("y", [TPC, D], F32, kind="ExternalOutput").ap()
    xs = dt_i("xs", [TOKW, D])
    xm = dt_i("xmid", [TOKW, D])
    on = dt_i("on", [TOKW, D], BF16)
    mod_loc = dt_i("mod_loc", [2, MOD_BLK * 512])
    mod_all = dt_i("mod_all", [2 * NCORES, MOD_BLK * 512])
    modT_i = [dt_i(f"modT_{i}", [128, 8, 8]) for i in range(DEPTH)]
    rows_i = [dt_i(f"rows_{i}", [6, D]) for i in range(DEPTH)]
    qT = dt_i("qT", [8, 128, TOKW], BF16)

    for t in range(0, NTT, 6):
        P.dma("sp", xs[t * 128:(t + 6) * 128, :], xin[t * 128:(t + 6) * 128, :], w=[("xs0", t)])
    emit_mod_phase(nc, P, cT, wblk, bblk, mod_loc, mod_all)
    for i in range(DEPTH):
        mix = LAYER_MIX[i]
        cfg = MIX[mix]
        nkp, vh, dv = cfg["nkp"], cfg["vh"], cfg["dv"]
        emit_consts(nc, P, i, mod_all, grow, modT_i[i], rows_i[i])
        kloc = dt_i(f"kloc{i}", [nkp, 128, TOKW], BF16)
        vloc = dt_i(f"vloc{i}", [vh, 128, NTT, dv + 1], BF16)
        with ExitStack() as es:
            P.es = es
            emit_pre(nc, P, mix, xs, wqkv[i], modT_i[i], gT_all[i], id_d, rc_d, rs_d, qT, kloc, vloc)
            _phase_end(P)
        P.es = P.es_root
        if mix == "A":
            kall = dt_i(f"kall{i}", [NCORES, nkp * 128, TOKW], BF16)
            vall = dt_i(f"vall{i}", [NCORES, vh * 128, NTT * (dv + 1)], BF16)
            P.cc(lambda e, kloc=kloc, kall=kall: e.collective_compute(
                "AllGather", ALU.bypass, replica_groups=[list(range(NCORES))],
                ins=[kloc.rearrange("a p t -> (a p) t").opt()], outs=[kall.rearrange("r q t -> (r q) t").opt()]),
                w=["kall"])
            P.cc(lambda e, vloc=vloc, vall=vall: e.collective_compute(
                "AllGather", ALU.bypass, replica_groups=[list(range(NCORES))],
                ins=[vloc.rearrange("h p c d -> (h p) (c d)").opt()], outs=[vall.rearrange("r q t -> (r q) t").opt()]),
                w=["vall"])
            _phase_end(P)
            lam_init = 0.8 - 0.6 * math.exp(-0.3 * i)
            with ExitStack() as es:
                P.es = es
                emit_att_A(nc, P, lam_init, qT, kloc, vloc, kall, vall, lam_d[i // 3], sub_d[i // 3], on)
                _phase_end(P)
            P.es = P.es_root
        else:
            HT = 2 if mix == "B" else 1
            nwin = (NCT + 2 * HT + NLT)
            kpack = dt_i(f"kpack{i}", [nkp, 128, 2 * HT * 128], BF16)
            vpack = dt_i(f"vpack{i}", [vh, 128, 2 * HT, dv + 1], BF16)
            kpall = dt_i(f"kpall{i}", [NCORES, nkp, 128, 2 * HT * 128], BF16)
            vpall = dt_i(f"vpall{i}", [NCORES, vh, 128, 2 * HT, dv + 1], BF16)
            kwin = dt_i(f"kwin{i}", [nkp, 128, nwin * 128], BF16)
            vwin = dt_i(f"vwin{i}", [vh, 128, nwin, dv + 1], BF16)
            emit_window(nc, P, mix, kloc, vloc, kpack, vpack, kpall, vpall, selB, kwin, vwin)
            with ExitStack() as es:
                P.es = es
                if mix == "B":
                    emit_att_B(nc, P, qT, kwin, vwin, bias_d, on)
                else:
                    emit_att_C(nc, P, qT, kwin, vwin, mk_d, sk_d, on)
                _phase_end(P)
            P.es = P.es_root
        emit_post(nc, P, xs, on, wo[i], w13[i], w2[i], modT_i[i], gT_all[i], rows_i[i], id_d, xm, xs)
    P.es = P.es_root
    for t in range(0, NLT, 4):
        P.dma("sp", y_d[t * 128:(t + 4) * 128, :], xs[t * 128:(t + 4) * 128, :], is_output=True)


def kernel(x, c, ctx, c_ctx, ada_w, ada_b, norm_g, ffn_w13, ffn_w2, a_wqkv, a_wo, a_lambda, a_subln,
           b_wqkv, b_wo, b_rpb, c_wqkv, c_wo, c_sink):
    f32 = lambda a: np.ascontiguousarray(np.asarray(a, dtype=np.float32))
    x = f32(x)[0]
    ctx = f32(ctx)[0]
    norm_g = f32(norm_g)
    ada_w, ada_b = f32(ada_w), f32(ada_b)
    cc_ = np.stack([f32(c).reshape(D), f32(c_ctx).reshape(D)], axis=0)
    cT = np.ascontiguousarray(cc_.reshape(2, 8, 128).transpose(2, 1, 0)).reshape(128, 16)
    wb = np.ascontiguousarray(ada_w.reshape(DEPTH, D, 12, 512).transpose(0, 2, 1, 3)).reshape(48, D, 512)
    bb = ada_b.reshape(48, 512)
    gT_all = np.ascontiguousarray(np.stack(
        [np.stack([_fm(norm_g[i, j]) for j in range(4)], axis=1) for i in range(DEPTH)], axis=0))
    grow = np.ascontiguousarray(norm_g[:, [1, 3], :])
    C, S = rope_tables()
    ident = np.eye(128, dtype=np.float32)
    bc = lambda v, n: np.ascontiguousarray(np.broadcast_to(np.asarray(v, np.float32).reshape(1, n), (128, n)))
    p = np.arange(128)
    m_lo = (p[:, None] >= p[None, :]).astype(np.float32)
    m_hi = (p[:, None] <= p[None, :]).astype(np.float32)
    shared = {
        "cT": cT, "gT_all": gT_all, "grow": grow, "ident": ident,
        "wqkv0": f32(a_wqkv[0]), "wqkv1": f32(b_wqkv[0]), "wqkv2": f32(c_wqkv[0]), "wqkv3": f32(a_wqkv[1]),
        "wo0": f32(a_wo[0]), "wo1": f32(b_wo[0]), "wo2": f32(c_wo[0]), "wo3": f32(a_wo[1]),
        "lam0": bc(a_lambda[0], 256), "lam1": bc(a_lambda[1], 256),
        "sub0": bc(a_subln[0], 128), "sub1": bc(a_subln[1], 128), "csink": bc(c_sink[0], 16),
    }
    for i in range(DEPTH):
        shared[f"w13_{i}"] = f32(ffn_w13[i])
        shared[f"w2_{i}"] = f32(ffn_w2[i])
    rpb = f32(b_rpb[0])
    in_maps = []
    for ci in range(NCORES):
        sl = slice(ci * MOD_BLK, (ci + 1) * MOD_BLK)
        brow = bb[sl].reshape(1, MOD_BLK * 512)
        sel = np.zeros((128, 16), np.float32)
        if ci > 0:
            sel[:, ci - 1] = 1.0
        if ci < NCORES - 1:
            sel[:, 8 + ci + 1] = 1.0
        masks = np.stack([m_lo, m_hi, m_lo if ci > 0 else np.zeros_like(m_lo),
                          m_hi if ci < NCORES - 1 else np.zeros_like(m_hi)], axis=1)
        m = dict(shared)
        m.update({
            "xin": np.ascontiguousarray(np.concatenate([x[ci * TPC:(ci + 1) * TPC], ctx], axis=0)),
            "wblk": np.ascontiguousarray(wb[sl]),
            "bblk": np.ascontiguousarray(np.concatenate([brow, brow], axis=0)),
            "ropeC": np.ascontiguousarray(C[ci * TPC:(ci + 1) * TPC]),
            "ropeS": np.ascontiguousarray(S[ci * TPC:(ci + 1) * TPC]),
            "nabias": np.ascontiguousarray(na_tables(rpb, ci).reshape(16, 128, 5 * 6 * 128)),
            "cmasks": np.ascontiguousarray(masks).astype(np.float32),
            "sel": sel,
        })
        in_maps.append(m)
    nc = bass.Bass("TRN2", target_bir_lowering=False, num_devices=NCORES)
    with ExitStack() as es:
        P = Prog(nc, es)
        build_fused(nc, P)
        P.finish()
    res = run_bass_kernel_spmd(nc, in_maps, core_ids=list(range(NCORES))).results
    y = np.concatenate([np.asarray(r["y"]) for r in res], axis=0)
    return np.ascontiguousarray(y[None]).astype(np.float32)
```
